# Optimizing a Trainium2 kernel written in Bass

```python
import math, functools
import jax, jax.numpy as jnp
from jax import lax
import numpy as np

D_MODEL = 4096
BATCH = 8
SEQ = 2048
DEPTH = 2

HEAD_DIM = 128
A_W = D_MODEL // 4
A_HEADS = A_W // HEAD_DIM
HG_CHUNK = 64
B_W = D_MODEL // 4
B_BLOCKS = B_W // HEAD_DIM
B_BD = B_W // B_BLOCKS
RG_CONV = 4
RG_C = 8.0
C_HEADS = D_MODEL // 2 // HEAD_DIM
C_KV = C_HEADS // 4
C_GROUP = C_HEADS // C_KV
C_W = C_HEADS * HEAD_DIM
Q_BLOCK = 128
ROPE_THETA = 10000.0
ROPE_HALF = HEAD_DIM // 2
GRID_W = 64
D_MIX = A_W + B_W + C_W
A_COLS = 5 * A_W
B_COLS = 2 * B_W
C_COLS = C_W + 2 * C_KV * HEAD_DIM
IN_COLS = A_COLS + B_COLS + C_COLS
D_FF = ((8 * D_MODEL // 3 + 255) // 256) * 256
FFN_CONV = 3
DEEPNORM_ALPHA = (2.0 * DEPTH) ** 0.25
DEEPNORM_BETA = (8.0 * DEPTH) ** -0.25
LN_EPS = 1e-5
RMS_EPS = 1e-6

kernel_name = "hybrid_hgrn2_rglru_axialgqa_encoder"


def layer_norm(x, w, b):
    xf = x.astype(jnp.float32)
    mu = jnp.mean(xf, axis=-1, keepdims=True)
    var = jnp.mean(jnp.square(xf - mu), axis=-1, keepdims=True)
    y = (xf - mu) * lax.rsqrt(var + LN_EPS)
    return (y * w.astype(jnp.float32) + b.astype(jnp.float32)).astype(x.dtype)


def rms_norm(x, w):
    xf = x.astype(jnp.float32)
    y = xf * lax.rsqrt(jnp.mean(jnp.square(xf), axis=-1, keepdims=True) + RMS_EPS)
    return y * w.astype(jnp.float32)


def dwconv(x, w, b, left):
    k_w = w.shape[0]
    s = x.shape[1]
    xp = jnp.pad(x, ((0, 0), (left, k_w - 1 - left), (0, 0)))
    y = b
    for k in range(k_w):
        y = y + xp[:, k:k + s, :] * w[k]
    return y


def hgrn2_scan(q, logf, k, v):
    bsz, s, h, dk = q.shape
    dv = v.shape[-1]
    n_chunks = s // HG_CHUNK

    def to_chunks(t):
        return t.reshape(bsz, n_chunks, HG_CHUNK, h, t.shape[-1]).transpose(1, 0, 3, 2, 4)

    tri = jnp.tril(jnp.ones((HG_CHUNK, HG_CHUNK), dtype=bool))

    def step(state, xs):
        qc, lfc, kc, vc = xs
        b = jnp.cumsum(lfc, axis=2)
        b_last = b[:, :, -1:, :]
        o_inter = jnp.einsum('bhcd,bhde->bhce', qc * jnp.exp(b), state)
        diff = b[:, :, :, None, :] - b[:, :, None, :, :]
        decay = jnp.exp(jnp.where(tri[:, :, None], diff, -jnp.inf))
        scores = jnp.einsum('bhid,bhijd,bhjd->bhij', qc, decay, kc)
        o_intra = jnp.einsum('bhij,bhje->bhie', scores, vc)
        new_state = (jnp.exp(b_last)[:, :, 0, :, None] * state
                     + jnp.einsum('bhjd,bhje->bhde', kc * jnp.exp(b_last - b), vc))
        return new_state, o_inter + o_intra

    state0 = jnp.zeros((bsz, h, dk, dv), jnp.float32)
    _, out = lax.scan(step, state0, (to_chunks(q), to_chunks(logf), to_chunks(k), to_chunks(v)))
    return out.transpose(1, 0, 3, 2, 4).reshape(bsz, s, h, dv)


def hgrn2_mixer(u, lb, norm_w):
    bsz, s, _ = u.shape
    heads = lambda t: t.reshape(bsz, s, A_HEADS, HEAD_DIM).astype(jnp.float32)
    q = heads(u[..., 0 * A_W:1 * A_W])
    zf_fwd = heads(u[..., 1 * A_W:2 * A_W])
    zf_bwd = heads(u[..., 2 * A_W:3 * A_W])
    iv = heads(u[..., 3 * A_W:4 * A_W])
    g = u[..., 4 * A_W:5 * A_W]

    def gates(z, lb_dir):
        lbd = lb_dir.reshape(A_HEADS, HEAD_DIM)
        logf = jnp.logaddexp(jnp.log(lbd), jnp.log1p(-lbd) + jax.nn.log_sigmoid(z))
        kk = (1.0 - lbd) * jax.nn.sigmoid(-z)
        return logf, kk

    lf_f, k_f = gates(zf_fwd, lb[0])
    lf_b, k_b = gates(zf_bwd, lb[1])
    flip = lambda t: jnp.flip(t, axis=1)
    o_f = hgrn2_scan(q, lf_f, k_f, iv)
    o_b = flip(hgrn2_scan(flip(q), flip(lf_b), flip(k_b), flip(iv)))
    o = rms_norm(o_f + o_b, norm_w.reshape(A_HEADS, HEAD_DIM)).reshape(bsz, s, A_W)
    return (o * jax.nn.silu(g.astype(jnp.float32))).astype(u.dtype)


def _linear_rec_combine(left, right):
    a1, b1 = left
    a2, b2 = right
    return a1 * a2, a2 * b1 + b2


def rglru_mixer(u, conv_w, conv_b, wa, ba, wx, bx, lam):
    bsz, s, _ = u.shape
    xb = u[..., :B_W]
    gate = u[..., B_W:]
    xc = dwconv(xb, conv_w, conv_b, left=RG_CONV // 2).astype(jnp.float32)
    xh = xc.reshape(bsz, s, B_BLOCKS, B_BD)

    def direction(d, reverse):
        r = jax.nn.sigmoid(jnp.einsum('bsnd,nde->bsne', xh, wa[d].astype(jnp.float32)).reshape(bsz, s, B_W)
                           + ba[d].astype(jnp.float32))
        ig = jax.nn.sigmoid(jnp.einsum('bsnd,nde->bsne', xh, wx[d].astype(jnp.float32)).reshape(bsz, s, B_W)
                            + bx[d].astype(jnp.float32))
        log_a = -RG_C * r * jax.nn.softplus(-lam[d].astype(jnp.float32))
        a = jnp.exp(log_a)
        bterm = jnp.sqrt(-jnp.expm1(2.0 * log_a)) * (ig * xc)
        _, hs = lax.associative_scan(_linear_rec_combine, (a, bterm), axis=1, reverse=reverse)
        return hs

    hsum = direction(0, False) + direction(1, True)
    return (jax.nn.gelu(gate.astype(jnp.float32)) * hsum).astype(u.dtype)


def axial_rope(t, cos, sin):
    q4 = ROPE_HALF // 2
    tr = t[..., :ROPE_HALF]
    tc = t[..., ROPE_HALF:]
    rot = jnp.concatenate([-tr[..., q4:], tr[..., :q4], -tc[..., q4:], tc[..., :q4]], axis=-1)
    return t * cos[:, None, :] + rot * sin[:, None, :]


def gqa_mixer(u, qn_w, kn_w, cos, sin):
    bsz, s, _ = u.shape
    kv_w = C_KV * HEAD_DIM
    q = u[..., :C_W].reshape(bsz, s, C_HEADS, HEAD_DIM)
    k = u[..., C_W:C_W + kv_w].reshape(bsz, s, C_KV, HEAD_DIM)
    v = u[..., C_W + kv_w:].reshape(bsz, s, C_KV, HEAD_DIM).astype(jnp.float32)
    q = axial_rope(rms_norm(q, qn_w), cos, sin)
    k = axial_rope(rms_norm(k, kn_w), cos, sin)
    scale = HEAD_DIM ** -0.5
    n_blk = s // Q_BLOCK
    qb = q.reshape(bsz, n_blk, Q_BLOCK, C_KV, C_GROUP, HEAD_DIM).transpose(1, 0, 2, 3, 4, 5)

    def attend(q_blk):
        sc = jnp.einsum('bqkgd,bskd->bkgqs', q_blk, k) * scale
        p = jax.nn.softmax(sc, axis=-1)
        return jnp.einsum('bkgqs,bskd->bqkgd', p, v)

    out = lax.map(attend, qb)
    return out.transpose(1, 0, 2, 3, 4, 5).reshape(bsz, s, C_W).astype(u.dtype)


def setup_inputs(seed: int = 0) -> dict:
    key = jax.random.key(seed)
    ks = jax.random.split(key, 28)
    f32 = jnp.float32

    def nrm(k, shape, scale):
        return jax.random.normal(k, shape, f32) * scale

    a8 = jax.random.uniform(ks[12], (DEPTH, 2, B_W), f32, 0.9, 0.999)
    a_base = a8 ** (1.0 / RG_C)
    rglru_lambda = jnp.log(a_base) - jnp.log1p(-a_base)
    return {
        "x": nrm(ks[0], (BATCH, SEQ, D_MODEL), 1.0),
        "emb_ln_w": 1.0 + nrm(ks[1], (D_MODEL,), 0.02),
        "emb_ln_b": nrm(ks[2], (D_MODEL,), 0.02),
        "w_in": nrm(ks[3], (DEPTH, D_MODEL, IN_COLS), D_MODEL ** -0.5),
        "hgrn_lb_logits": nrm(ks[4], (DEPTH, 2, A_W), 0.5),
        "hgrn_norm_w": 1.0 + nrm(ks[5], (DEPTH, A_W), 0.02),
        "rglru_conv_w": nrm(ks[6], (DEPTH, RG_CONV, B_W), RG_CONV ** -0.5),
        "rglru_conv_b": nrm(ks[7], (DEPTH, B_W), 0.02),
        "rglru_wa": nrm(ks[8], (DEPTH, 2, B_BLOCKS, B_BD, B_BD), B_BD ** -0.5),
        "rglru_ba": nrm(ks[9], (DEPTH, 2, B_W), 0.02),
        "rglru_wx": nrm(ks[10], (DEPTH, 2, B_BLOCKS, B_BD, B_BD), B_BD ** -0.5),
        "rglru_bx": nrm(ks[11], (DEPTH, 2, B_W), 0.02),
        "rglru_lambda": rglru_lambda,
        "attn_q_norm_w": 1.0 + nrm(ks[13], (DEPTH, HEAD_DIM), 0.02),
        "attn_k_norm_w": 1.0 + nrm(ks[14], (DEPTH, HEAD_DIM), 0.02),
        "w_out": nrm(ks[15], (DEPTH, D_MIX, D_MODEL), D_MIX ** -0.5 * DEEPNORM_BETA),
        "ln1_w": 1.0 + nrm(ks[16], (DEPTH, D_MODEL), 0.02),
        "ln1_b": nrm(ks[17], (DEPTH, D_MODEL), 0.02),
        "ffn_w_up": nrm(ks[18], (DEPTH, D_MODEL, 2 * D_FF), D_MODEL ** -0.5),
        "ffn_conv_w": nrm(ks[19], (DEPTH, FFN_CONV, D_FF), FFN_CONV ** -0.5),
        "ffn_conv_b": nrm(ks[20], (DEPTH, D_FF), 0.02),
        "ffn_w_down": nrm(ks[21], (DEPTH, D_FF, D_MODEL), D_FF ** -0.5 * DEEPNORM_BETA),
        "ln2_w": 1.0 + nrm(ks[22], (DEPTH, D_MODEL), 0.02),
        "ln2_b": nrm(ks[23], (DEPTH, D_MODEL), 0.02),
    }


def reference(x, emb_ln_w, emb_ln_b, w_in, hgrn_lb_logits, hgrn_norm_w, rglru_conv_w, rglru_conv_b,
              rglru_wa, rglru_ba, rglru_wx, rglru_bx, rglru_lambda, attn_q_norm_w, attn_k_norm_w,
              w_out, ln1_w, ln1_b, ffn_w_up, ffn_conv_w, ffn_conv_b, ffn_w_down, ln2_w, ln2_b):
    s = x.shape[1]
    rows = s // GRID_W
    g_r, g_c = jnp.meshgrid(jnp.arange(rows), jnp.arange(GRID_W), indexing='ij')
    row = g_r.reshape(s).astype(jnp.float32)
    col = g_c.reshape(s).astype(jnp.float32)
    inv_freq = ROPE_THETA ** (-jnp.arange(0, ROPE_HALF, 2, dtype=jnp.float32) / ROPE_HALF)
    ang_r = row[:, None] * inv_freq[None, :]
    ang_c = col[:, None] * inv_freq[None, :]
    ang = jnp.concatenate([ang_r, ang_r, ang_c, ang_c], axis=-1)
    cos, sin = jnp.cos(ang), jnp.sin(ang)

    lb_cs = jnp.cumsum(jax.nn.softmax(hgrn_lb_logits.astype(jnp.float32), axis=0), axis=0)
    lower_bounds = lb_cs - lb_cs[0:1]

    h = layer_norm(x, emb_ln_w, emb_ln_b)
    for l in range(DEPTH):
        u = h @ w_in[l]
        u_a = u[..., :A_COLS]
        u_b = u[..., A_COLS:A_COLS + B_COLS]
        u_c = u[..., A_COLS + B_COLS:]
        y_a = hgrn2_mixer(u_a, lower_bounds[l], hgrn_norm_w[l])
        y_b = rglru_mixer(u_b, rglru_conv_w[l], rglru_conv_b[l], rglru_wa[l], rglru_ba[l],
                          rglru_wx[l], rglru_bx[l], rglru_lambda[l])
        y_c = gqa_mixer(u_c, attn_q_norm_w[l], attn_k_norm_w[l], cos, sin)
        mix = jnp.concatenate([y_a, y_b, y_c], axis=-1) @ w_out[l]
        h = layer_norm(DEEPNORM_ALPHA * h + mix, ln1_w[l], ln1_b[l])
        gu = h @ ffn_w_up[l]
        gate = jax.nn.silu(dwconv(gu[..., :D_FF], ffn_conv_w[l], ffn_conv_b[l], left=FFN_CONV // 2))
        ffn = (gate * gu[..., D_FF:]) @ ffn_w_down[l]
        h = layer_norm(DEEPNORM_ALPHA * h + ffn, ln2_w[l], ln2_b[l])
    return h
```

```python
import numpy as np
from contextlib import ExitStack
import concourse.bass as bass
import concourse.mybir as mybir
from concourse.bass_utils import run_bass_kernel_spmd

AF = mybir.ActivationFunctionType
ALU = mybir.AluOpType
DT = mybir.dt
F32 = DT.float32
BF16 = DT.bfloat16

SAME_ENG_SYNC = True

S = 2048
D = 4096
KT = D // 128
DEPTH = 2
A_W = 1024
B_W = 1024
C_W = 2048
A_COLS = 5 * A_W
B_COLS = 2 * B_W
IN_COLS = 10240
D_FF = 11008
FT = D_FF // 128
ALPHA = (2.0 * DEPTH) ** 0.25
LN_EPS = 1e-5
RMS_EPS = 1e-6


class Op:
    __slots__ = ("eng", "fn", "deps", "key", "count", "used", "idx")


class Buf:
    __slots__ = ("w", "r", "name", "excl")

    def __init__(self, name="", excl=False):
        self.w = None
        self.r = {}
        self.name = name
        self.excl = excl


def _semkey(op):
    return ("E_" + op.eng) if op.key is None else ("D_" + op.key)


class Prog:
    ENGS = ("sync", "act", "pool", "dve", "pe")

    def __init__(self, nc):
        self.nc = nc
        self.ops = {e: [] for e in self.ENGS}
        self.n = 0
        self.latest = {}
        self.pending = {e: None for e in self.ENGS}
        self.on_barrier = None

    def add(self, eng, fn, deps=(), key=None):
        op = Op()
        op.eng = eng
        op.fn = fn
        op.key = key
        op.count = None
        op.used = False
        dl = []
        pend = self.pending[eng]
        if pend is not None:
            deps = list(deps) + pend
            self.pending[eng] = None
        for d in deps:
            if d is None:
                continue
            if d.key is None and d.eng == eng and (eng == "pe" or not SAME_ENG_SYNC):
                continue
            d.used = True
            dl.append(d)
        op.deps = dl
        op.idx = self.n
        self.n += 1
        self.ops[eng].append(op)
        self.latest[_semkey(op)] = op
        return op

    def op(self, eng, fn, reads=(), writes=(), key=None, deps=()):
        dl = list(deps)
        ex = [b for b in reads if b.excl]
        if ex:
            reads = [b for b in reads if not b.excl]
            writes = list(writes) + ex
        for b in reads:
            if b.w is not None:
                dl.append(b.w)
        for b in writes:
            if b.w is not None:
                dl.append(b.w)
            dl.extend(b.r.values())
        o = self.add(eng, fn, dl, key)
        sk = _semkey(o)
        for b in reads:
            b.r[sk] = o
        for b in writes:
            b.w = o
            b.r = {}
        return o

    def dma(self, q, out, in_, reads=(), writes=(), key=None, deps=()):
        assert key is not None
        return self.op(q, lambda e: e.dma_start(out=out, in_=in_), reads, writes, key, deps)

    def barrier(self):
        if self.on_barrier is not None:
            self.on_barrier()
        last = list(self.latest.values())
        for e in self.ENGS:
            self.pending[e] = list(last)

    def emit(self, final_waits=()):
        nc = self.nc
        cnt = {}
        for e in self.ENGS:
            for op in self.ops[e]:
                if op.key is None and not op.used:
                    continue
                k = _semkey(op)
                cnt[k] = cnt.get(k, 0) + (1 if op.key is None else 16)
                op.count = cnt[k]
        with ExitStack() as st:
            sems = {k: st.enter_context(nc.semaphore("s_" + k)) for k in cnt}
            block = st.enter_context(nc.Block())

            def run_engine(ename, eng):
                waited = {}
                for op in self.ops[ename]:
                    need = {}
                    for d in op.deps:
                        k = _semkey(d)
                        if d.count > need.get(k, 0):
                            need[k] = d.count
                    for k, v in need.items():
                        if waited.get(k, 0) >= v:
                            continue
                        eng.wait_ge(sems[k], v)
                        waited[k] = v
                    ins = op.fn(eng)
                    if op.count is not None:
                        ins.then_inc(sems[_semkey(op)], 1 if op.key is None else 16)
                if ename == "sync":
                    need = {}
                    for d in final_waits:
                        k = _semkey(d)
                        need[k] = max(need.get(k, 0), d.count)
                    for k, v in need.items():
                        eng.wait_ge(sems[k], v)

            @block.sync
            def _(e):
                run_engine("sync", e)

            @block.scalar
            def _(e):
                run_engine("act", e)

            @block.gpsimd
            def _(e):
                run_engine("pool", e)

            @block.vector
            def _(e):
                run_engine("dve", e)

            @block.tensor
            def _(e):
                run_engine("pe", e)


def _pp_layout():
    off = {}
    n = 0

    def put(name, c):
        nonlocal n
        off[name] = n
        n += c
    put("lb_logits", DEPTH * 2 * 8)
    for l in range(DEPTH):
        put(f"conv_w{l}", 4 * 8)
        put(f"conv_b{l}", 8)
        put(f"ba{l}", 16)
        put(f"bx{l}", 16)
        put(f"lam{l}", 16)
        put(f"hnw{l}", 8)
        put(f"qnw{l}", 1)
        put(f"knw{l}", 1)
        put(f"ln1w{l}", 32)
        put(f"ln1b{l}", 32)
        put(f"ln2w{l}", 32)
        put(f"ln2b{l}", 32)
        put(f"fcw{l}", 3 * FT)
        put(f"fcb{l}", FT)
    return off, n


PP_OFF, PP_N = _pp_layout()


def _pack_params(inp):
    pp = np.zeros((128, PP_N), np.float32)

    def cols(name, arr):
        a = np.asarray(arr, np.float32)
        lead = int(np.prod(a.shape[:-1])) if a.ndim > 1 else 1
        a = a.reshape(lead, -1, 128)
        a = a.transpose(2, 0, 1).reshape(128, -1)
        pp[:, PP_OFF[name]:PP_OFF[name] + a.shape[1]] = a
    cols("lb_logits", inp["hgrn_lb_logits"])
    for l in range(DEPTH):
        cols(f"conv_w{l}", inp["rglru_conv_w"][l])
        cols(f"conv_b{l}", inp["rglru_conv_b"][l])
        cols(f"ba{l}", inp["rglru_ba"][l])
        cols(f"bx{l}", inp["rglru_bx"][l])
        cols(f"lam{l}", inp["rglru_lambda"][l])
        cols(f"hnw{l}", inp["hgrn_norm_w"][l])
        cols(f"qnw{l}", inp["attn_q_norm_w"][l])
        cols(f"knw{l}", inp["attn_k_norm_w"][l])
        cols(f"ln1w{l}", inp["ln1_w"][l])
        cols(f"ln1b{l}", inp["ln1_b"][l])
        cols(f"ln2w{l}", inp["ln2_w"][l])
        cols(f"ln2b{l}", inp["ln2_b"][l])
        cols(f"fcw{l}", inp["ffn_conv_w"][l])
        cols(f"fcb{l}", inp["ffn_conv_b"][l])
    return pp


def _rope_tables():
    rows = S // 64
    t = np.arange(S)
    row = (t // 64).astype(np.float32)
    col = (t % 64).astype(np.float32)
    inv_freq = (np.float32(10000.0) ** (-np.arange(0, 64, 2, dtype=np.float32) / np.float32(64))).astype(np.float32)
    ang_r = row[:, None] * inv_freq[None, :]
    ang_c = col[:, None] * inv_freq[None, :]
    ang = np.concatenate([ang_r, ang_r, ang_c, ang_c], axis=-1).astype(np.float32)
    return np.ascontiguousarray(np.stack([np.cos(ang).T, np.sin(ang).T]).astype(np.float32))


def rev_ap(ap2d):
    a = ap2d.ap
    n = ap2d.shape[1]
    return bass.AP(ap2d.tensor, ap2d.offset + (n - 1) * a[1][0], [list(a[0]), [-a[1][0], n]])


def bcast_last(ap3, n):
    a = ap3.ap
    return bass.AP(ap3.tensor, ap3.offset, [list(a[0]), list(a[1]), [0, n]])


def build(nc, phases=None, dump=(), inject=(), layers=(0, 1)):
    def on(name):
        return phases is None or name in phases

    def dram_in(name, shape, dt=F32):
        return nc.dram_tensor(name, list(shape), dt, kind="ExternalInput").ap()

    x = dram_in("x", [S, D])
    emb_w = dram_in("emb_ln_w", [D])
    emb_b = dram_in("emb_ln_b", [D])
    w_in = dram_in("w_in", [DEPTH, D, IN_COLS]) if on("win") else None
    w_out = dram_in("w_out", [DEPTH, D, D]) if on("wout") else None
    w_up = dram_in("ffn_w_up", [DEPTH, D, 2 * D_FF]) if on("wup") else None
    w_down = dram_in("ffn_w_down", [DEPTH, D_FF, D]) if on("wdown") else None
    wa_d = dram_in("rglru_wa", [DEPTH, 2, 8, 128, 128])
    wx_d = dram_in("rglru_wx", [DEPTH, 2, 8, 128, 128])
    pp_d = dram_in("pp", [128, PP_N])
    rope_d = dram_in("rope", [2, 128, S])
    y_out = nc.dram_tensor("y", [S, D], F32, kind="ExternalOutput").ap()

    def scratch(name, shape, dt):
        kind = "Internal"
        if name in dump:
            kind = "ExternalOutput"
        if name in inject:
            kind = "ExternalInput"
        return nc.dram_tensor(name, list(shape), dt, kind=kind).ap()

    hT_f = scratch("hT_f", [D, S], F32)
    hT_b = scratch("hT_b", [D, S], BF16)
    uT = scratch("uT", [IN_COLS, S], F32)
    mixT = scratch("mixT", [D, S], BF16)
    zT = scratch("zT", [D, S], F32)
    gT = scratch("gT", [D_FF, S], BF16)
    wdT = scratch("wdT", [D // 128, 128, FT * 128], BF16)
    B_wd = Buf("wdT")
    B_hT_f, B_hT_b, B_uT, B_mixT, B_zT, B_gT = (Buf(n) for n in ("hT_f", "hT_b", "uT", "mixT", "zT", "gT"))

    P = Prog(nc)
    final = []
    with ExitStack() as glob:
        def gsb(name, shape, dt):
            return glob.enter_context(nc.sbuf_tensor(name, shape, dt))
        uid = [0]

        def _bump():
            uid[0] += 1
        P.on_barrier = _bump

        def sbt(name, shape, dt):
            return nc.sbuf_tensor(f"{name}_u{uid[0]}", shape, dt)
        ps = [glob.enter_context(nc.psum_tensor(f"ps{i}", [128, 512], F32)) for i in range(8)]
        PB = [Buf(f"ps{i}", excl=True) for i in range(8)]
        ident = gsb("ident", [128, 128], F32)
        ones_f = gsb("ones_f", [128, 128], F32)
        ones_b = gsb("ones_b", [128, 128], BF16)
        pp = gsb("pp_sb", [128, PP_N], F32)
        B_const = Buf("const")

        def pc(name, i=0):
            o = PP_OFF[name] + i
            return pp[:, o:o + 1]

        P.dma("sync", pp[:], pp_d, writes=[B_const], key="pp")
        P.op("pool", lambda e: e.memset(ident[:], 0.0), writes=[B_const])
        P.op("pool", lambda e: e.affine_select(out=ident[:], in_=ident[:], pattern=[[-1, 128]], compare_op=ALU.not_equal,
                                               fill=1.0, base=0, channel_multiplier=1), writes=[B_const])
        P.op("pool", lambda e: e.memset(ones_f[:], 1.0), writes=[B_const])
        P.op("pool", lambda e: e.memset(ones_b[:], 1.0), writes=[B_const])
        P.barrier()

        def phase_embed():
            with ExitStack() as ph:
                def sb(name, shape, dt):
                    return ph.enter_context(sbt(name, shape, dt))
                xt = [sb(f"e_xt{i}", [128, D], F32) for i in range(2)]
                XT = [Buf() for _ in range(2)]
                wB = sb("e_wB", [128, D], F32)
                bB = sb("e_bB", [128, D], F32)
                WB = Buf()
                stg_f = [sb(f"e_sf{i}", [128, KT, 128], F32) for i in range(2)]
                stg_b = [sb(f"e_sb{i}", [128, KT, 128], BF16) for i in range(2)]
                SF = [Buf() for _ in range(2)]
                SBb = [Buf() for _ in range(2)]
                stats = sb("e_stats", [128, 8, 6], F32)
                mv = sb("e_mv", [128, 2], F32)
                rstd = sb("e_rstd", [128, 1], F32)
                ST = Buf()
                P.dma("sync", wB[:], emb_w.partition_broadcast(128), writes=[WB], key="e_wB")
                P.dma("sync", bB[:], emb_b.partition_broadcast(128), writes=[WB], key="e_bB")
                hTf_v = hT_f.rearrange("(kt p) n -> p kt n", p=128)
                hTb_v = hT_b.rearrange("(kt p) n -> p kt n", p=128)
                bank = 0
                import os
                for t in range(int(os.environ.get("EMB_TILES", S // 128))):
                    b = t % 2
                    P.dma("sync", xt[b][:], x[t * 128:(t + 1) * 128, :], writes=[XT[b]], key=f"e_xt{b}")
                    for c in range(8):
                        P.op("dve", lambda e, b=b, c=c: e.bn_stats(out=stats[:, c, :], in_=xt[b][:, c * 512:(c + 1) * 512]),
                             reads=[XT[b]], writes=[ST])
                    P.op("dve", lambda e: e.bn_aggr(out=mv[:], in_=stats[:]), writes=[ST])
                    P.op("act", lambda e: e.activation(out=rstd[:], in_=mv[:, 1:2], func=AF.Sqrt, bias=LN_EPS, scale=1.0), writes=[ST])
                    P.op("dve", lambda e: e.reciprocal(out=rstd[:], in_=rstd[:]), writes=[ST])
                    P.op("dve", lambda e, b=b: e.tensor_scalar(out=xt[b][:], in0=xt[b][:], scalar1=mv[:, 0:1], scalar2=rstd[:, 0:1],
                                                               op0=ALU.subtract, op1=ALU.mult), reads=[ST], writes=[XT[b]])
                    P.op("pool", lambda e, b=b: e.tensor_tensor(out=xt[b][:], in0=xt[b][:], in1=wB[:], op=ALU.mult), reads=[WB], writes=[XT[b]])
                    P.op("pool", lambda e, b=b: e.tensor_tensor(out=xt[b][:], in0=xt[b][:], in1=bB[:], op=ALU.add), reads=[WB], writes=[XT[b]])
                    for g in range(8):
                        bk = bank % 8
                        bank += 1
                        for i in range(4):
                            k = g * 4 + i
                            P.op("pe", lambda e, bk=bk, i=i, k=k, b=b: e.transpose(ps[bk][:, i * 128:(i + 1) * 128], xt[b][:, k * 128:(k + 1) * 128], ident[:]),
                                 reads=[XT[b], B_const], writes=[PB[bk]])
                        P.op("act", lambda e, bk=bk, g=g, b=b: e.copy(out=stg_f[b][:, g * 4:(g + 1) * 4, :], in_=ps[bk][:].rearrange("p (a n) -> p a n", a=4)),
                             reads=[PB[bk]], writes=[SF[b]])
                        if not os.environ.get("EMB_NOBF"):
                            P.op("dve", lambda e, bk=bk, g=g, b=b: e.tensor_copy(out=stg_b[b][:, g * 4:(g + 1) * 4, :], in_=ps[bk][:].rearrange("p (a n) -> p a n", a=4)),
                                 reads=[PB[bk]], writes=[SBb[b]])
                    P.dma("act", hTf_v[:, :, t * 128:(t + 1) * 128], stg_f[b][:], reads=[SF[b]], writes=[B_hT_f], key=f"e_sf{b}")
                    if not os.environ.get("EMB_NOBF"):
                        P.dma(os.environ.get("EMB_Q", "act"), hTb_v[:, :, t * 128:(t + 1) * 128], stg_b[b][:], reads=[SBb[b]], writes=[B_hT_b], key=f"e_sb{b}")
            P.barrier()

        def load_act(ph, name, src, kt, tok0, ntok, Bsrc):
            t = ph.enter_context(sbt(name, [128, kt, ntok], BF16))
            Bt = Buf()
            v = src.rearrange("(kt p) n -> p kt n", p=128)
            nsp = 4
            step = (kt + nsp - 1) // nsp
            for i in range(nsp):
                k0, k1 = i * step, min(kt, (i + 1) * step)
                if k0 >= k1:
                    continue
                P.dma("sync" if i % 2 == 0 else "act", t[:, k0:k1, :], v[:, k0:k1, tok0:tok0 + ntok], reads=[Bsrc], writes=[Bt], key=f"{name}_{i}")
            return t, Bt

        def mm_stream(ph, tag, actT, Bact, kt, ntok, wsrc, col_tiles, evac, cw, nwb):
            wv = wsrc.rearrange("(kt p) n -> p kt n", p=128)
            wbuf = [ph.enter_context(sbt(f"{tag}_w{i}", [128, kt, cw], BF16)) for i in range(nwb)]
            WBs = [Buf() for _ in range(nwb)]
            nj = ntok // 512
            per = cw // 128
            nblk = len(col_tiles) // per
            bankctr = 0
            for blk in range(nblk):
                wb = wbuf[blk % nwb]
                Bw = WBs[blk % nwb]
                c_first = col_tiles[blk * per]
                P.dma("pool", wb[:], wv[:, :, c_first:c_first + cw], writes=[Bw], key=f"{tag}_w{blk % nwb}")
                for ct in range(per):
                    ci = blk * per + ct
                    banks = [(bankctr + j) % 8 for j in range(nj)]
                    bankctr += nj
                    for k in range(kt):
                        for j in range(nj):
                            P.op("pe", lambda e, bk=banks[j], k=k, ct=ct, j=j, wb=wb: e.matmul(
                                ps[bk][:], wb[:, k, ct * 128:(ct + 1) * 128], actT[:, k, j * 512:(j + 1) * 512],
                                start=(k == 0), stop=(k == kt - 1)), reads=[Bw, Bact], writes=[PB[banks[j]]])
                    evac(ci, col_tiles[ci], banks)

        def phase_win(l):
            with ExitStack() as ph:
                actT, Bact = load_act(ph, "wi_act", hT_b, KT, 0, S, B_hT_b)
                stg = [ph.enter_context(sbt(f"wi_stg{i}", [128, S], F32)) for i in range(2)]
                SG = [Buf() for _ in range(2)]

                def evac(ci, c0, banks):
                    s = ci % 2
                    for j, bk in enumerate(banks):
                        if j % 2 == 0:
                            P.op("act", lambda e, bk=bk, j=j, s=s: e.copy(out=stg[s][:, j * 512:(j + 1) * 512], in_=ps[bk][:]), reads=[PB[bk]], writes=[SG[s]])
                        else:
                            P.op("dve", lambda e, bk=bk, j=j, s=s: e.tensor_copy(out=stg[s][:, j * 512:(j + 1) * 512], in_=ps[bk][:]), reads=[PB[bk]], writes=[SG[s]])
                    P.dma("sync", uT[c0:c0 + 128, :], stg[s][:], reads=[SG[s]], writes=[B_uT], key=f"wi_stg{s}")
                mm_stream(ph, "wi", actT, Bact, KT, S, w_in[l], [c * 128 for c in range(IN_COLS // 128)], evac, 256, 3)
            P.barrier()

        def phase_wout(l):
            with ExitStack() as ph:
                actT, Bact = load_act(ph, "wo_act", mixT, KT, 0, S, B_mixT)
                stg = [ph.enter_context(sbt(f"wo_stg{i}", [128, S], F32)) for i in range(2)]
                hres = [ph.enter_context(sbt(f"wo_hr{i}", [128, S], F32)) for i in range(2)]
                SG = [Buf() for _ in range(2)]
                HR = [Buf() for _ in range(2)]

                def evac(ci, c0, banks):
                    s = ci % 2
                    P.dma("sync", hres[s][:], hT_f[c0:c0 + 128, :], reads=[B_hT_f], writes=[HR[s]], key=f"wo_hr{s}")
                    for j, bk in enumerate(banks):
                        P.op("dve", lambda e, bk=bk, j=j, s=s: e.scalar_tensor_tensor(
                            out=stg[s][:, j * 512:(j + 1) * 512], in0=hres[s][:, j * 512:(j + 1) * 512], scalar=float(ALPHA),
                            in1=ps[bk][:], op0=ALU.mult, op1=ALU.add), reads=[PB[bk], HR[s]], writes=[SG[s]])
                    P.dma("sync", zT[c0:c0 + 128, :], stg[s][:], reads=[SG[s]], writes=[B_zT], key=f"wo_stg{s}")
                mm_stream(ph, "wo", actT, Bact, KT, S, w_out[l], [c * 128 for c in range(D // 128)], evac, 256, 2)
            P.barrier()

        def phase_ln(wname, bname, last):
            with ExitStack() as ph:
                def sb(name, shape, dt):
                    return ph.enter_context(sbt(name, shape, dt))
                zt = [sb(f"ln_z{i}", [128, S], F32) for i in range(3)]
                ZT = [Buf() for _ in range(3)]
                sq = [sb(f"ln_sq{i}", [128, S], F32) for i in range(2)]
                SQ = [Buf() for _ in range(2)]
                mean = sb("ln_mean", [128, S], F32)
                rstd = sb("ln_rstd", [128, S], F32)
                MS = Buf()
                t1 = [sb(f"ln_t1{i}", [128, S], F32) for i in range(2)]
                T1 = [Buf() for _ in range(2)]
                hb = [sb(f"ln_hb{i}", [128, S], BF16) for i in range(2)]
                HB = [Buf() for _ in range(2)]
                ystg = [sb(f"ln_y{i}", [128, 16, 128], F32) for i in range(2)] if last else None
                YS = [Buf() for _ in range(2)]
                for k in range(KT):
                    b = k % 3
                    P.dma("sync", zt[b][:], zT[k * 128:(k + 1) * 128, :], reads=[B_zT], writes=[ZT[b]], key=f"ln_z{b}")
                    P.op("act", lambda e, b=b, k=k: e.activation(out=sq[k % 2][:], in_=zt[b][:], func=AF.Square), reads=[ZT[b]], writes=[SQ[k % 2]])
                    for j in range(4):
                        P.op("pe", lambda e, j=j, b=b, k=k: e.matmul(ps[j][:], ones_f[:], zt[b][:, j * 512:(j + 1) * 512], start=(k == 0), stop=(k == KT - 1)),
                             reads=[ZT[b], B_const], writes=[PB[j]])
                    for j in range(4):
                        P.op("pe", lambda e, j=j, k=k: e.matmul(ps[4 + j][:], ones_f[:], sq[k % 2][:, j * 512:(j + 1) * 512], start=(k == 0), stop=(k == KT - 1)),
                             reads=[SQ[k % 2], B_const], writes=[PB[4 + j]])
                for j in range(4):
                    sl = slice(j * 512, (j + 1) * 512)
                    P.op("act", lambda e, j=j, sl=sl: e.mul(out=mean[:, sl], in_=ps[j][:], mul=1.0 / D), reads=[PB[j]], writes=[MS])
                    P.op("act", lambda e, j=j, sl=sl: e.mul(out=rstd[:, sl], in_=ps[4 + j][:], mul=1.0 / D), reads=[PB[4 + j]], writes=[MS])
                P.op("dve", lambda e: e.tensor_tensor(out=t1[0][:], in0=mean[:], in1=mean[:], op=ALU.mult), reads=[MS], writes=[T1[0]])
                P.op("dve", lambda e: e.tensor_tensor(out=rstd[:], in0=rstd[:], in1=t1[0][:], op=ALU.subtract), reads=[T1[0]], writes=[MS])
                P.op("act", lambda e: e.activation(out=rstd[:], in_=rstd[:], func=AF.Sqrt, bias=LN_EPS, scale=1.0), writes=[MS])
                P.op("dve", lambda e: e.reciprocal(out=rstd[:], in_=rstd[:]), writes=[MS])
                bank = 0
                yv = y_out.rearrange("(t p) f -> p t f", p=128)
                for k in range(KT):
                    b = k % 3
                    s = k % 2
                    P.dma("sync", zt[b][:], zT[k * 128:(k + 1) * 128, :], reads=[B_zT], writes=[ZT[b]], key=f"ln_z{b}")
                    P.op("dve", lambda e, b=b, s=s: e.tensor_tensor(out=t1[s][:], in0=zt[b][:], in1=mean[:], op=ALU.subtract), reads=[ZT[b], MS], writes=[T1[s]])
                    P.op("pool", lambda e, s=s: e.tensor_tensor(out=t1[s][:], in0=t1[s][:], in1=rstd[:], op=ALU.mult), reads=[MS], writes=[T1[s]])
                    P.op("act", lambda e, s=s, k=k: e.activation(out=t1[s][:], in_=t1[s][:], func=AF.Identity, bias=pc(bname, k), scale=pc(wname, k)),
                         reads=[B_const], writes=[T1[s]])
                    if not last:
                        P.op("act", lambda e, s=s: e.copy(out=hb[s][:], in_=t1[s][:]), reads=[T1[s]], writes=[HB[s]])
                        P.dma("act", hT_f[k * 128:(k + 1) * 128, :], t1[s][:], reads=[T1[s]], writes=[B_hT_f], key=f"ln_t1{s}")
                        P.dma("act", hT_b[k * 128:(k + 1) * 128, :], hb[s][:], reads=[HB[s]], writes=[B_hT_b], key=f"ln_hb{s}")
                    else:
                        for g in range(4):
                            bk = bank % 8
                            bank += 1
                            for i in range(4):
                                tt = g * 4 + i
                                P.op("pe", lambda e, bk=bk, i=i, tt=tt, s=s: e.transpose(ps[bk][:, i * 128:(i + 1) * 128], t1[s][:, tt * 128:(tt + 1) * 128], ident[:]),
                                     reads=[T1[s], B_const], writes=[PB[bk]])
                            eng = "act" if g % 2 == 0 else "dve"
                            if eng == "act":
                                P.op("act", lambda e, bk=bk, g=g, s=s: e.copy(out=ystg[s][:, g * 4:(g + 1) * 4, :], in_=ps[bk][:].rearrange("p (a n) -> p a n", a=4)),
                                     reads=[PB[bk]], writes=[YS[s]])
                            else:
                                P.op("dve", lambda e, bk=bk, g=g, s=s: e.tensor_copy(out=ystg[s][:, g * 4:(g + 1) * 4, :], in_=ps[bk][:].rearrange("p (a n) -> p a n", a=4)),
                                     reads=[PB[bk]], writes=[YS[s]])
                        d = P.dma("act", yv[:, :, k * 128:(k + 1) * 128], ystg[s][:], reads=[YS[s]], key=f"ln_y{s}")
                        final.append(d)
            P.barrier()

        def phase_wup(l):
            with ExitStack() as ph:
                def sb(name, shape, dt):
                    return ph.enter_context(sbt(name, shape, dt))
                actT, Bact = load_act(ph, "wu_act", hT_b, KT, 0, S, B_hT_b)
                wg = [sb(f"wu_wg{i}", [128, KT, 128], BF16) for i in range(2)]
                wu = [sb(f"wu_wu{i}", [128, KT, 128], BF16) for i in range(2)]
                WG = [Buf() for _ in range(2)]
                WU = [Buf() for _ in range(2)]
                gsbuf = sb("wu_gs", [128, S + 2], F32)
                GS = Buf()
                cv = sb("wu_cv", [128, S], F32)
                CV = Buf()
                ob = [sb(f"wu_ob{i}", [128, S], BF16) for i in range(2)]
                OB = [Buf() for _ in range(2)]
                P.op("pool", lambda e: e.memset(gsbuf[:, 0:1], 0.0), writes=[GS])
                P.op("pool", lambda e: e.memset(gsbuf[:, S + 1:S + 2], 0.0), writes=[GS])
                wv = w_up[l].rearrange("(kt p) n -> p kt n", p=128)
                for m in range(FT):
                    b = m % 2
                    P.dma("pool", wg[b][:], wv[:, :, m * 128:(m + 1) * 128], writes=[WG[b]], key=f"wu_wg{b}")
                    P.dma("pool", wu[b][:], wv[:, :, D_FF + m * 128:D_FF + (m + 1) * 128], writes=[WU[b]], key=f"wu_wu{b}")
                    for k in range(KT):
                        for j in range(4):
                            P.op("pe", lambda e, j=j, k=k, b=b: e.matmul(ps[j][:], wg[b][:, k, :], actT[:, k, j * 512:(j + 1) * 512], start=(k == 0), stop=(k == KT - 1)),
                                 reads=[WG[b], Bact], writes=[PB[j]])
                    for j in range(4):
                        P.op("act", lambda e, j=j: e.copy(out=gsbuf[:, 1 + j * 512:1 + (j + 1) * 512], in_=ps[j][:]), reads=[PB[j]], writes=[GS])
                    for k in range(KT):
                        for j in range(4):
                            P.op("pe", lambda e, j=j, k=k, b=b: e.matmul(ps[4 + j][:], wu[b][:, k, :], actT[:, k, j * 512:(j + 1) * 512], start=(k == 0), stop=(k == KT - 1)),
                                 reads=[WU[b], Bact], writes=[PB[4 + j]])
                    P.op("dve", lambda e, m=m: e.tensor_scalar(out=cv[:], in0=gsbuf[:, 1:S + 1], scalar1=pc(f"fcw{l}", FT + m), scalar2=pc(f"fcb{l}", m),
                                                               op0=ALU.mult, op1=ALU.add), reads=[GS, B_const], writes=[CV])
                    P.op("dve", lambda e, m=m: e.scalar_tensor_tensor(out=cv[:], in0=gsbuf[:, 0:S], scalar=pc(f"fcw{l}", m), in1=cv[:],
                                                                      op0=ALU.mult, op1=ALU.add), reads=[GS, B_const], writes=[CV])
                    P.op("dve", lambda e, m=m: e.scalar_tensor_tensor(out=cv[:], in0=gsbuf[:, 2:S + 2], scalar=pc(f"fcw{l}", 2 * FT + m), in1=cv[:],
                                                                      op0=ALU.mult, op1=ALU.add), reads=[GS, B_const], writes=[CV])
                    P.op("act", lambda e: e.activation(out=cv[:], in_=cv[:], func=AF.Silu), writes=[CV])
                    for j in range(4):
                        P.op("dve", lambda e, j=j, b=b: e.tensor_tensor(out=ob[b][:, j * 512:(j + 1) * 512], in0=cv[:, j * 512:(j + 1) * 512], in1=ps[4 + j][:], op=ALU.mult),
                             reads=[CV, PB[4 + j]], writes=[OB[b]])
                    P.dma("sync", gT[m * 128:(m + 1) * 128, :], ob[b][:], reads=[OB[b]], writes=[B_gT], key=f"wu_ob{b}")
            P.barrier()

        def phase_wdown(l):
            with ExitStack() as ph:
                def sb(name, shape, dt):
                    return ph.enter_context(sbt(name, shape, dt))
                TB = 512
                gblk = sb("wd_g", [128, FT, TB], BF16)
                GB = Buf()
                wbuf = [sb(f"wd_w{i}", [128, FT, 128], BF16) for i in range(3)]
                WBs = [Buf() for _ in range(3)]
                hres = [sb(f"wd_hr{i}", [128, TB], F32) for i in range(2)]
                HR = [Buf() for _ in range(2)]
                stg = [sb(f"wd_stg{i}", [128, TB], F32) for i in range(2)]
                SG = [Buf() for _ in range(2)]
                gv = gT.rearrange("(kt p) n -> p kt n", p=128)
                wv = w_down[l].rearrange("(kt p) n -> p kt n", p=128)
                ctr = 0
                for tb in range(S // TB):
                    t0 = tb * TB
                    for i, (k0, k1) in enumerate(((0, 22), (22, 44), (44, 66), (66, FT))):
                        P.dma("sync", gblk[:, k0:k1, :], gv[:, k0:k1, t0:t0 + TB], reads=[B_gT], writes=[GB], key=f"wd_g{i}")
                    for c in range(D // 128):
                        wb = ctr % 3
                        bk = ctr % 8
                        s = ctr % 2
                        ctr += 1
                        P.dma("act", wbuf[wb][:].rearrange("p k n -> p (k n)"), wdT[c], reads=[B_wd], writes=[WBs[wb]], key=f"wd_w{wb}")
                        P.dma("sync", hres[s][:], hT_f[c * 128:(c + 1) * 128, t0:t0 + TB], reads=[B_hT_f], writes=[HR[s]], key=f"wd_hr{s}")
                        for k in range(FT):
                            P.op("pe", lambda e, bk=bk, k=k, wb=wb: e.matmul(ps[bk][:], wbuf[wb][:, k, :], gblk[:, k, :], start=(k == 0), stop=(k == FT - 1)),
                                 reads=[WBs[wb], GB], writes=[PB[bk]])
                        P.op("dve", lambda e, bk=bk, s=s: e.scalar_tensor_tensor(out=stg[s][:], in0=hres[s][:], scalar=float(ALPHA), in1=ps[bk][:],
                                                                                 op0=ALU.mult, op1=ALU.add), reads=[PB[bk], HR[s]], writes=[SG[s]])
                        P.dma("sync", zT[c * 128:(c + 1) * 128, t0:t0 + TB], stg[s][:], reads=[SG[s]], writes=[B_zT], key=f"wd_stg{s}")
            P.barrier()

        def wd_convert(sb, tag, l, c0, c1):
            cvt = [sb(f"{tag}_cv{i}", [128, FT, 128], BF16) for i in range(2)]
            CVB = [Buf() for _ in range(2)]
            wdv = w_down[l].rearrange("(kt p) n -> p kt n", p=128)
            for c in range(c0, c1):
                b = c % 2
                P.dma("pool", cvt[b][:], wdv[:, :, c * 128:(c + 1) * 128], writes=[CVB[b]], key=f"{tag}_cv{b}")
                P.dma("pool", wdT[c], cvt[b][:].rearrange("p k n -> p (k n)"), reads=[CVB[b]], writes=[B_wd], key=f"{tag}_cvs{b}")

        def phase_rglru(l):
            with ExitStack() as ph:
                def sb(name, shape, dt=F32):
                    return ph.enter_context(sbt(name, shape, dt))
                xb = [sb(f"rg_xb{i}", [128, S]) for i in range(2)]
                gt = [sb(f"rg_gt{i}", [128, S]) for i in range(2)]
                XB = [Buf() for _ in range(2)]
                GT = [Buf() for _ in range(2)]
                names = ["XC", "R", "IG", "A", "M", "BT", "HS0", "HS1", "G2"]
                T = {n: sb("rg_" + n, [128, S]) for n in names}
                Bf = {n: Buf() for n in names}
                yb = [sb(f"rg_y{i}", [128, S], BF16) for i in range(2)]
                YB = [Buf() for _ in range(2)]
                wa = sb("rg_wa", [128, 16, 128])
                wx = sb("rg_wx", [128, 16, 128])
                cdp = sb("rg_cd", [128, 16])
                PR = Buf()
                P.dma("sync", wa[:], wa_d[l].rearrange("r n d e -> d (r n) e"), writes=[PR], key="rg_wa")
                P.dma("sync", wx[:], wx_d[l].rearrange("r n d e -> d (r n) e"), writes=[PR], key="rg_wx")
                lo = PP_OFF[f"lam{l}"]
                P.op("act", lambda e: e.activation(out=cdp[:], in_=pp[:, lo:lo + 16], func=AF.Exp, scale=-1.0), reads=[B_const], writes=[PR])
                P.op("act", lambda e: e.activation(out=cdp[:], in_=cdp[:], func=AF.Ln, bias=1.0), writes=[PR])
                P.op("dve", lambda e: e.tensor_scalar(out=cdp[:], in0=cdp[:], scalar1=-8.0, scalar2=None, op0=ALU.mult), writes=[PR])
                if w_down is not None:
                    wd_convert(sb, "rg", l, 0, 16)
                r0 = A_COLS
                for n in range(8):
                    b = n % 2
                    P.dma("sync", xb[b][:], uT[r0 + n * 128:r0 + (n + 1) * 128, :], reads=[B_uT], writes=[XB[b]], key=f"rg_xb{b}")
                    P.dma("sync", gt[b][:], uT[r0 + B_W + n * 128:r0 + B_W + (n + 1) * 128, :], reads=[B_uT], writes=[GT[b]], key=f"rg_gt{b}")
                    XC = T["XC"]
                    cws = [pc(f"conv_w{l}", k * 8 + n) for k in range(4)]
                    cbn = pc(f"conv_b{l}", n)
                    P.op("dve", lambda e, b=b, w2=cws[2], cbn=cbn: e.tensor_scalar(out=XC[:], in0=xb[b][:], scalar1=w2, scalar2=cbn, op0=ALU.mult, op1=ALU.add),
                         reads=[XB[b], B_const], writes=[Bf["XC"]])
                    P.op("dve", lambda e, b=b, w=cws[0]: e.scalar_tensor_tensor(out=XC[:, 2:S], in0=xb[b][:, 0:S - 2], scalar=w, in1=XC[:, 2:S], op0=ALU.mult, op1=ALU.add),
                         reads=[XB[b]], writes=[Bf["XC"]])
                    P.op("dve", lambda e, b=b, w=cws[1]: e.scalar_tensor_tensor(out=XC[:, 1:S], in0=xb[b][:, 0:S - 1], scalar=w, in1=XC[:, 1:S], op0=ALU.mult, op1=ALU.add),
                         reads=[XB[b]], writes=[Bf["XC"]])
                    P.op("dve", lambda e, b=b, w=cws[3]: e.scalar_tensor_tensor(out=XC[:, 0:S - 1], in0=xb[b][:, 1:S], scalar=w, in1=XC[:, 0:S - 1], op0=ALU.mult, op1=ALU.add),
                         reads=[XB[b]], writes=[Bf["XC"]])
                    for d in range(2):
                        HS = T[f"HS{d}"]
                        for j in range(4):
                            P.op("pe", lambda e, j=j, d=d, n=n: e.matmul(ps[j][:], wa[:, d * 8 + n, :], XC[:, j * 512:(j + 1) * 512], start=True, stop=True),
                                 reads=[Bf["XC"], PR], writes=[PB[j]])
                        for j in range(4):
                            P.op("pe", lambda e, j=j, d=d, n=n: e.matmul(ps[4 + j][:], wx[:, d * 8 + n, :], XC[:, j * 512:(j + 1) * 512], start=True, stop=True),
                                 reads=[Bf["XC"], PR], writes=[PB[4 + j]])
                        for j in range(4):
                            P.op("act", lambda e, j=j, d=d, n=n: e.activation(out=T["R"][:, j * 512:(j + 1) * 512], in_=ps[j][:], func=AF.Sigmoid, bias=pc(f"ba{l}", d * 8 + n)),
                                 reads=[PB[j], B_const], writes=[Bf["R"]])
                        for j in range(4):
                            P.op("act", lambda e, j=j, d=d, n=n: e.activation(out=T["IG"][:, j * 512:(j + 1) * 512], in_=ps[4 + j][:], func=AF.Sigmoid, bias=pc(f"bx{l}", d * 8 + n)),
                                 reads=[PB[4 + j], B_const], writes=[Bf["IG"]])
                        P.op("act", lambda e, d=d, n=n: e.activation(out=T["A"][:], in_=T["R"][:], func=AF.Exp, scale=cdp[:, d * 8 + n:d * 8 + n + 1]),
                             reads=[Bf["R"], PR], writes=[Bf["A"]])
                        P.op("act", lambda e: e.activation(out=T["M"][:], in_=T["A"][:], func=AF.Square), reads=[Bf["A"]], writes=[Bf["M"]])
                        P.op("dve", lambda e: e.tensor_scalar(out=T["M"][:], in0=T["M"][:], scalar1=-1.0, scalar2=1.0, op0=ALU.mult, op1=ALU.add), writes=[Bf["M"]])
                        P.op("act", lambda e: e.activation(out=T["M"][:], in_=T["M"][:], func=AF.Sqrt), writes=[Bf["M"]])
                        P.op("dve", lambda e: e.tensor_tensor(out=T["BT"][:], in0=T["IG"][:], in1=XC[:], op=ALU.mult), reads=[Bf["IG"], Bf["XC"]], writes=[Bf["BT"]])
                        P.op("dve", lambda e: e.tensor_tensor(out=T["BT"][:], in0=T["BT"][:], in1=T["M"][:], op=ALU.mult), reads=[Bf["M"]], writes=[Bf["BT"]])
                        if d == 0:
                            P.op("dve", lambda e, HS=HS: e.tensor_tensor_scan(out=HS[:], data0=T["A"][:], data1=T["BT"][:], initial=0.0, op0=ALU.mult, op1=ALU.add),
                                 reads=[Bf["A"], Bf["BT"]], writes=[Bf["HS0"]])
                        else:
                            P.op("dve", lambda e, HS=HS: e.tensor_tensor_scan(out=rev_ap(HS[:]), data0=rev_ap(T["A"][:]), data1=rev_ap(T["BT"][:]), initial=0.0,
                                                                              op0=ALU.mult, op1=ALU.add), reads=[Bf["A"], Bf["BT"]], writes=[Bf["HS1"]])
                    P.op("dve", lambda e: e.tensor_tensor(out=T["HS0"][:], in0=T["HS0"][:], in1=T["HS1"][:], op=ALU.add), reads=[Bf["HS1"]], writes=[Bf["HS0"]])
                    G2 = T["G2"]
                    P.op("act", lambda e, b=b: e.activation(out=G2[:], in_=gt[b][:], func=AF.Square), reads=[GT[b]], writes=[Bf["G2"]])
                    P.op("dve", lambda e: e.tensor_scalar(out=G2[:], in0=G2[:], scalar1=0.044715, scalar2=1.0, op0=ALU.mult, op1=ALU.add), writes=[Bf["G2"]])
                    P.op("dve", lambda e, b=b: e.tensor_tensor(out=G2[:], in0=G2[:], in1=gt[b][:], op=ALU.mult), reads=[GT[b]], writes=[Bf["G2"]])
                    P.op("act", lambda e: e.activation(out=G2[:], in_=G2[:], func=AF.Sigmoid, scale=1.5957691216057308), writes=[Bf["G2"]])
                    P.op("dve", lambda e, b=b: e.tensor_tensor(out=G2[:], in0=G2[:], in1=gt[b][:], op=ALU.mult), reads=[GT[b]], writes=[Bf["G2"]])
                    P.op("dve", lambda e, b=b: e.tensor_tensor(out=yb[b][:], in0=G2[:], in1=T["HS0"][:], op=ALU.mult), reads=[Bf["G2"], Bf["HS0"]], writes=[YB[b]])
                    P.dma("act", mixT[A_W + n * 128:A_W + (n + 1) * 128, :], yb[b][:], reads=[YB[b]], writes=[B_mixT], key=f"rg_y{b}")
            P.barrier()

        def phase_hgrn(l):
            with ExitStack() as ph:
                def sb(name, shape, dt=F32):
                    return ph.enter_context(sbt(name, shape, dt))
                inn = ["q", "zf", "zb", "iv", "g"]
                IN = [{n: sb(f"hg_{n}{i}", [128, S]) for n in inn} for i in range(2)]
                BIN = [{n: Buf() for n in inn} for i in range(2)]
                names = ["F", "KK", "LF", "B", "E"]
                T = {n: sb("hg_" + n, [128, S]) for n in names}
                Bf = {n: Buf() for n in names}
                vtok = sb("hg_vtok", [128, 16, 128], BF16)
                khtok = sb("hg_khtok", [128, 16, 128], BF16)
                VT, KH = Buf(), Buf()
                Eb = sb("hg_Eb", [128, S], BF16)
                Fb = sb("hg_Fb", [128, S], BF16)
                BEb, BFb = Buf(), Buf()
                Sb = [sb(f"hg_Sb{i}", [128, 128], BF16) for i in range(4)]
                SSb = [Buf() for _ in range(4)]
                oacc = sb("hg_oacc", [128, S])
                OA = Buf()
                Sst = [sb(f"hg_S{i}", [128, 128]) for i in range(2)]
                SS = [Buf() for _ in range(2)]
                stm = [sb(f"hg_stm{i}", [128, 128], BF16) for i in range(2)]
                STM = [Buf() for _ in range(2)]
                dec = sb("hg_dec", [128, 32])
                DEC = Buf()
                mk = [sb(f"hg_mk{i}", [128, 128]) for i in range(2)]
                smk = [sb(f"hg_smk{i}", [128, S]) for i in range(2)]
                lbp = sb("hg_lb", [128, 16])
                oml = sb("hg_oml", [128, 16])
                e01 = sb("hg_e01", [128, 32])
                MK = Buf()
                yb = [sb(f"hg_y{i}", [128, S], BF16) for i in range(2)]
                YB = [Buf() for _ in range(2)]
                P.op("pool", lambda e: e.memset(mk[0][:], 1.0), writes=[MK])
                P.op("pool", lambda e: e.affine_select(out=mk[0][:], in_=mk[0][:], pattern=[[1, 128]], compare_op=ALU.is_ge, fill=0.0, base=0, channel_multiplier=-1), writes=[MK])
                P.op("pool", lambda e: e.memset(mk[0][0:64, 64:128], 0.0), writes=[MK])
                P.op("pool", lambda e: e.memset(mk[1][:], 1.0), writes=[MK])
                P.op("pool", lambda e: e.affine_select(out=mk[1][:], in_=mk[1][:], pattern=[[-1, 128]], compare_op=ALU.is_ge, fill=0.0, base=0, channel_multiplier=1), writes=[MK])
                P.op("pool", lambda e: e.memset(mk[1][64:128, 0:64], 0.0), writes=[MK])
                P.op("pool", lambda e: e.memset(smk[0][:], 1.0), writes=[MK])
                P.op("pool", lambda e: e.memset(smk[0][:].rearrange("p (c k) -> p c k", k=64)[:, :, 0:1], 0.0), writes=[MK])
                P.op("pool", lambda e: e.memset(smk[1][:], 1.0), writes=[MK])
                P.op("pool", lambda e: e.memset(smk[1][:].rearrange("p (c k) -> p c k", k=64)[:, :, 63:64], 0.0), writes=[MK])
                lo = PP_OFF["lb_logits"]
                P.op("act", lambda e: e.activation(out=e01[:], in_=pp[:, lo:lo + 32], func=AF.Exp), reads=[B_const], writes=[MK])
                P.op("dve", lambda e: e.tensor_tensor(out=oml[:], in0=e01[:, 0:16], in1=e01[:, 16:32], op=ALU.add), writes=[MK])
                P.op("dve", lambda e: e.reciprocal(out=oml[:], in_=oml[:]), writes=[MK])
                if l == 0:
                    P.op("dve", lambda e: e.tensor_tensor(out=lbp[:], in0=e01[:, 0:16], in1=oml[:], op=ALU.mult), writes=[MK])
                    P.op("dve", lambda e: e.tensor_tensor(out=lbp[:], in0=lbp[:], in1=lbp[:], op=ALU.subtract), writes=[MK])
                else:
                    P.op("dve", lambda e: e.tensor_tensor(out=lbp[:], in0=e01[:, 16:32], in1=oml[:], op=ALU.mult), writes=[MK])
                P.op("dve", lambda e: e.tensor_scalar(out=oml[:], in0=lbp[:], scalar1=-1.0, scalar2=1.0, op0=ALU.mult, op1=ALU.add), writes=[MK])

                def v3(t):
                    return t[:].rearrange("p (c k) -> p c k", k=64)
                trbank = 0
                for hd in range(8):
                    ib = hd % 2
                    I = IN[ib]
                    BI = BIN[ib]
                    for ni, n in enumerate(inn):
                        P.dma("sync", I[n][:], uT[ni * A_W + hd * 128:ni * A_W + (hd + 1) * 128, :], reads=[B_uT], writes=[BI[n]], key=f"hg_{n}{ib}")
                    for g4 in range(4):
                        bk = 6 + (trbank % 2)
                        trbank += 1
                        for i in range(4):
                            tt = g4 * 4 + i
                            P.op("pe", lambda e, bk=bk, i=i, tt=tt, I=I: e.transpose(ps[bk][:, i * 128:(i + 1) * 128], I["iv"][:, tt * 128:(tt + 1) * 128], ident[:]),
                                 reads=[BI["iv"], B_const], writes=[PB[bk]])
                        P.op("act", lambda e, bk=bk, g4=g4: e.copy(out=vtok[:, g4 * 4:(g4 + 1) * 4, :], in_=ps[bk][:].rearrange("p (a n) -> p a n", a=4)),
                             reads=[PB[bk]], writes=[VT])
                    for d in range(2):
                        z = I["zf"] if d == 0 else I["zb"]
                        BZ = BI["zf"] if d == 0 else BI["zb"]
                        col = d * 8 + hd
                        F_, KK, LF, B_, E_ = T["F"], T["KK"], T["LF"], T["B"], T["E"]
                        P.op("act", lambda e, z=z: e.activation(out=F_[:], in_=z[:], func=AF.Sigmoid), reads=[BZ], writes=[Bf["F"]])
                        P.op("dve", lambda e, col=col: e.tensor_scalar(out=F_[:], in0=F_[:], scalar1=oml[:, col:col + 1], scalar2=lbp[:, col:col + 1], op0=ALU.mult, op1=ALU.add),
                             reads=[MK], writes=[Bf["F"]])
                        P.op("pool", lambda e: e.tensor_scalar(out=KK[:], in0=F_[:], scalar1=-1.0, scalar2=1.0, op0=ALU.mult, op1=ALU.add), reads=[Bf["F"]], writes=[Bf["KK"]])
                        P.op("act", lambda e: e.activation(out=LF[:], in_=F_[:], func=AF.Ln), reads=[Bf["F"]], writes=[Bf["LF"]])
                        if d == 0:
                            P.op("dve", lambda e: e.tensor_tensor_scan(out=B_[:], data0=smk[0][:], data1=LF[:], initial=0.0, op0=ALU.mult, op1=ALU.add),
                                 reads=[Bf["LF"], MK], writes=[Bf["B"]])
                        else:
                            P.op("dve", lambda e: e.tensor_tensor_scan(out=rev_ap(B_[:]), data0=rev_ap(smk[1][:]), data1=rev_ap(LF[:]), initial=0.0, op0=ALU.mult, op1=ALU.add),
                                 reads=[Bf["LF"], MK], writes=[Bf["B"]])
                        P.op("act", lambda e: e.activation(out=E_[:], in_=B_[:], func=AF.Exp), reads=[Bf["B"]], writes=[Bf["E"]])
                        P.op("dve", lambda e, I=I: e.tensor_tensor(out=Eb[:], in0=I["q"][:], in1=E_[:], op=ALU.mult), reads=[BI["q"], Bf["E"]], writes=[BEb])
                        P.op("act", lambda e: e.activation(out=F_[:], in_=B_[:], func=AF.Exp, scale=-1.0), reads=[Bf["B"], Bf["KK"], Bf["LF"]], writes=[Bf["F"]])
                        P.op("pool", lambda e: e.tensor_tensor(out=Fb[:], in0=KK[:], in1=F_[:], op=ALU.mult), reads=[Bf["KK"], Bf["F"]], writes=[BFb])
                        ecol = 63 if d == 0 else 0
                        P.op("dve", lambda e, ecol=ecol: e.tensor_tensor(out=v3(LF), in0=bcast_last(v3(B_)[:, :, ecol:ecol + 1], 64), in1=v3(B_), op=ALU.subtract),
                             reads=[Bf["B"]], writes=[Bf["LF"]])
                        P.op("act", lambda e: e.activation(out=LF[:], in_=LF[:], func=AF.Exp), writes=[Bf["LF"]])
                        P.op("pool", lambda e: e.tensor_tensor(out=LF[:], in0=KK[:], in1=LF[:], op=ALU.mult), reads=[Bf["KK"]], writes=[Bf["LF"]])
                        P.op("act", lambda e, ecol=ecol: e.activation(out=dec[:], in_=v3(B_)[:, :, ecol], func=AF.Exp), reads=[Bf["B"]], writes=[DEC])
                        for g4 in range(4):
                            bk = 6 + (trbank % 2)
                            trbank += 1
                            for i in range(4):
                                tt = g4 * 4 + i
                                P.op("pe", lambda e, bk=bk, i=i, tt=tt: e.transpose(ps[bk][:, i * 128:(i + 1) * 128], LF[:, tt * 128:(tt + 1) * 128], ident[:]),
                                     reads=[Bf["LF"], B_const], writes=[PB[bk]])
                            P.op("dve", lambda e, bk=bk, g4=g4: e.tensor_copy(out=khtok[:, g4 * 4:(g4 + 1) * 4, :], in_=ps[bk][:].rearrange("p (a n) -> p a n", a=4)),
                                 reads=[PB[bk]], writes=[KH])
                        P.op("pool", lambda e: e.memset(Sst[0][:], 0.0), writes=[SS[0]])
                        P.op("pool", lambda e: e.memset(Sb[0][:], 0.0), writes=[SSb[0]])
                        sstep = 0
                        prs = list(range(16)) if d == 0 else list(range(15, -1, -1))
                        ccs = (0, 1) if d == 0 else (1, 0)

                        def scores(pi):
                            pr = prs[pi]
                            sbk = pi % 2
                            tk = slice(pr * 128, (pr + 1) * 128)
                            P.op("pe", lambda e, sbk=sbk, tk=tk: e.matmul(ps[sbk][:, 0:128], Fb[:, tk], Eb[:, tk], start=True, stop=True),
                                 reads=[BFb, BEb], writes=[PB[sbk]])
                            P.op("dve", lambda e, sbk=sbk, d=d: e.tensor_tensor(out=stm[sbk][:], in0=ps[sbk][:, 0:128], in1=mk[d][:], op=ALU.mult),
                                 reads=[PB[sbk], MK], writes=[STM[sbk]])
                        scores(0)
                        for pi, pr in enumerate(prs):
                            if pi + 1 < 16:
                                scores(pi + 1)
                            sbk = pi % 2
                            obk = 2 + pi % 2
                            tk = slice(pr * 128, (pr + 1) * 128)
                            base = sstep
                            for ci, cc in enumerate(ccs):
                                c = pr * 2 + cc
                                ck = slice(cc * 64, (cc + 1) * 64)
                                dbk = 4 + (c % 2)
                                n_ = base + ci
                                fi, fo = n_ % 2, (n_ + 1) % 2
                                P.op("pe", lambda e, dbk=dbk, ck=ck, pr=pr: e.matmul(ps[dbk][:, 0:128], khtok[ck, pr, :], vtok[ck, pr, :], start=True, stop=True),
                                     reads=[KH, VT], writes=[PB[dbk]])
                                P.op("dve", lambda e, dbk=dbk, c=c, fi=fi, fo=fo: e.scalar_tensor_tensor(out=Sst[fo][:], in0=Sst[fi][:], scalar=dec[:, c:c + 1], in1=ps[dbk][:, 0:128], op0=ALU.mult, op1=ALU.add),
                                     reads=[PB[dbk], DEC, SS[fi]], writes=[SS[fo]])
                                P.op("act", lambda e, fo=fo, n_=n_: e.copy(out=Sb[(n_ + 1) % 4][:], in_=Sst[fo][:]), reads=[SS[fo]], writes=[SSb[(n_ + 1) % 4]])
                            for ci, cc in enumerate(ccs):
                                ck = slice(cc * 64, (cc + 1) * 64)
                                tck = slice(pr * 128 + cc * 64, pr * 128 + (cc + 1) * 64)
                                n_ = base + ci
                                P.op("pe", lambda e, obk=obk, ck=ck, pr=pr, sbk=sbk: e.matmul(ps[obk][:, ck], vtok[:, pr, :], stm[sbk][:, ck], start=True, stop=False),
                                     reads=[VT, STM[sbk]], writes=[PB[obk]])
                                P.op("pe", lambda e, obk=obk, ck=ck, tck=tck, n_=n_: e.matmul(ps[obk][:, ck], Sb[n_ % 4][:], Eb[:, tck], start=False, stop=True),
                                     reads=[SSb[n_ % 4], BEb], writes=[PB[obk]])
                            sstep = base + 2
                            if d == 0:
                                P.op("act", lambda e, obk=obk, tk=tk: e.copy(out=oacc[:, tk], in_=ps[obk][:, 0:128]), reads=[PB[obk]], writes=[OA])
                            else:
                                P.op("dve", lambda e, obk=obk, tk=tk: e.tensor_tensor(out=oacc[:, tk], in0=oacc[:, tk], in1=ps[obk][:, 0:128], op=ALU.add), reads=[PB[obk]], writes=[OA])
                    F_, KK, LF, B_, E_ = T["F"], T["KK"], T["LF"], T["B"], T["E"]
                    P.op("act", lambda e: e.activation(out=E_[:], in_=oacc[:], func=AF.Square), reads=[OA, Bf["F"]], writes=[Bf["E"]])
                    for j in range(4):
                        P.op("pe", lambda e, j=j: e.matmul(ps[j][:], ones_f[:], E_[:, j * 512:(j + 1) * 512], start=True, stop=True), reads=[Bf["E"], B_const], writes=[PB[j]])
                        P.op("act", lambda e, j=j: e.activation(out=F_[:, j * 512:(j + 1) * 512], in_=ps[j][:], func=AF.Sqrt, bias=RMS_EPS, scale=1.0 / 128), reads=[PB[j]], writes=[Bf["F"]])
                    P.op("dve", lambda e: e.reciprocal(out=F_[:], in_=F_[:]), writes=[Bf["F"]])
                    P.op("dve", lambda e: e.tensor_tensor(out=F_[:], in0=oacc[:], in1=F_[:], op=ALU.mult), reads=[OA], writes=[Bf["F"]])
                    P.op("act", lambda e, I=I: e.activation(out=KK[:], in_=I["g"][:], func=AF.Silu), reads=[BI["g"]], writes=[Bf["KK"]])
                    P.op("dve", lambda e, ib=ib, hd=hd: e.scalar_tensor_tensor(out=yb[ib][:], in0=F_[:], scalar=pc(f"hnw{l}", hd), in1=KK[:], op0=ALU.mult, op1=ALU.mult),
                         reads=[Bf["F"], Bf["KK"], B_const], writes=[YB[ib]])
                    P.dma("act", mixT[hd * 128:(hd + 1) * 128, :], yb[ib][:], reads=[YB[ib]], writes=[B_mixT], key=f"hg_y{ib}")
            P.barrier()

        def phase_attn(l):
            with ExitStack() as ph:
                def sb(name, shape, dt=F32):
                    return ph.enter_context(sbt(name, shape, dt))
                cosT = sb("at_cos", [128, S])
                sinT = sb("at_sin", [128, S])
                Rm = sb("at_R", [128, 128])
                CS = Buf()
                P.dma("sync", cosT[:], rope_d[0], writes=[CS], key="at_cos")
                P.dma("sync", sinT[:], rope_d[1], writes=[CS], key="at_sin")
                P.op("pool", lambda e: e.memset(Rm[:], 0.0), writes=[CS])
                for (c0, base, fill) in ((0, -32, -1.0), (32, 0, 1.0), (64, -96, -1.0), (96, -64, 1.0)):
                    P.op("pool", lambda e, c0=c0, base=base, fill=fill: e.affine_select(out=Rm[:, c0:c0 + 32], in_=Rm[:, c0:c0 + 32], pattern=[[-1, 32]],
                                                                                      compare_op=ALU.not_equal, fill=fill, base=base, channel_multiplier=1), writes=[CS])
                if w_down is not None:
                    wd_convert(sb, "at", l, 16, 32)
                xin = [sb(f"at_x{i}", [128, S]) for i in range(2)]
                XI = [Buf() for _ in range(2)]
                SQs = [sb(f"at_sq{i}", [128, S]) for i in range(2)]
                RSs = [sb(f"at_rs{i}", [128, S]) for i in range(2)]
                QNs = [sb(f"at_qn{i}", [128, S]) for i in range(2)]
                BSQs, BRSs, BQNs = [Buf(), Buf()], [Buf(), Buf()], [Buf(), Buf()]
                nrc = [0]
                KR = sb("at_kr", [128, S], BF16)
                BKR = Buf()
                QR = [sb(f"at_qr{i}", [128, S], BF16) for i in range(4)]
                BQR = [Buf() for _ in range(4)]
                vtok = sb("at_vtok", [128, 16, 128], BF16)
                VT = Buf()
                pT = [sb(f"at_p{i}", [128, 512], BF16) for i in range(4)]
                PT = [Buf() for _ in range(4)]
                SBK = (0, 1, 2, 7)
                rden = [sb(f"at_rd{i}", [128, 512]) for i in range(2)]
                RD = [Buf() for _ in range(2)]
                ost = [sb(f"at_o{i}", [128, 512], BF16) for i in range(2)]
                OS = [Buf() for _ in range(2)]
                c_base = A_COLS + B_COLS
                xctr = [0]

                def x_load(row0):
                    b = xctr[0] % 2
                    xctr[0] += 1
                    P.dma("sync", xin[b][:], uT[row0:row0 + 128, :], reads=[B_uT], writes=[XI[b]], key=f"at_x{b}")
                    return b

                def norm_rope(b, nwname, out_t, Bout):
                    X = xin[b]
                    ts_ = nrc[0] % 2
                    nrc[0] += 1
                    SQ, RS, QN = SQs[ts_], RSs[ts_], QNs[ts_]
                    BSQ, BRS, BQN = BSQs[ts_], BRSs[ts_], BQNs[ts_]
                    P.op("act", lambda e: e.activation(out=SQ[:], in_=X[:], func=AF.Square), reads=[XI[b]], writes=[BSQ])
                    for j in range(4):
                        P.op("pe", lambda e, j=j: e.matmul(ps[j][:], ones_f[:], SQ[:, j * 512:(j + 1) * 512], start=True, stop=True), reads=[BSQ, B_const], writes=[PB[j]])
                        P.op("act", lambda e, j=j: e.activation(out=RS[:, j * 512:(j + 1) * 512], in_=ps[j][:], func=AF.Sqrt, bias=RMS_EPS, scale=1.0 / 128), reads=[PB[j]], writes=[BRS])
                    P.op("dve", lambda e: e.reciprocal(out=RS[:], in_=RS[:]), writes=[BRS])
                    P.op("dve", lambda e: e.scalar_tensor_tensor(out=QN[:], in0=X[:], scalar=pc(nwname), in1=RS[:], op0=ALU.mult, op1=ALU.mult),
                         reads=[XI[b], BRS, B_const], writes=[BQN])
                    for j in range(4):
                        sl = slice(j * 512, (j + 1) * 512)
                        P.op("pe", lambda e, j=j, sl=sl: e.matmul(ps[4 + j][:], Rm[:], QN[:, sl], start=True, stop=True), reads=[BQN, CS], writes=[PB[4 + j]])
                        P.op("dve", lambda e, j=j, sl=sl: e.tensor_tensor(out=SQ[:, sl], in0=ps[4 + j][:], in1=sinT[:, sl], op=ALU.mult), reads=[PB[4 + j], CS], writes=[BSQ])
                    P.op("dve", lambda e: e.tensor_tensor(out=QN[:], in0=QN[:], in1=cosT[:], op=ALU.mult), reads=[CS], writes=[BQN])
                    P.op("dve", lambda e: e.tensor_tensor(out=out_t[:], in0=QN[:], in1=SQ[:], op=ALU.add), reads=[BQN, BSQ], writes=[Bout])

                scale = 128.0 ** -0.5
                octr = 0
                items = []
                for kv in range(4):
                    items.append(c_base + C_W + kv * 128)
                    items.append(c_base + C_W + 512 + kv * 128)
                    for g in range(4):
                        items.append(c_base + (kv * 4 + g) * 128)
                nxt = [0]
                pend = [x_load(items[0])]

                def take():
                    b = pend.pop(0)
                    nxt[0] += 1
                    if nxt[0] < len(items):
                        pend.append(x_load(items[nxt[0]]))
                    return b
                for kv in range(4):
                    norm_rope(take(), f"knw{l}", KR, BKR)
                    b = take()
                    for g4 in range(4):
                        bk = 4 + g4
                        for i in range(4):
                            tt = g4 * 4 + i
                            P.op("pe", lambda e, bk=bk, i=i, tt=tt, b=b: e.transpose(ps[bk][:, i * 128:(i + 1) * 128], xin[b][:, tt * 128:(tt + 1) * 128], ident[:]),
                                 reads=[XI[b], B_const], writes=[PB[bk]])
                        P.op("act", lambda e, bk=bk, g4=g4: e.copy(out=vtok[:, g4 * 4:(g4 + 1) * 4, :], in_=ps[bk][:].rearrange("p (a n) -> p a n", a=4)),
                             reads=[PB[bk]], writes=[VT])
                    for g in range(4):
                        norm_rope(take(), f"qnw{l}", QR[g], BQR[g])
                    for g in range(4):
                        hq = kv * 4 + g
                        qb = g
                        for qc in range(4):
                            qs = slice(qc * 512, (qc + 1) * 512)
                            ob = 3 + (octr % 2)
                            db = 5 + (octr % 2)
                            o2 = octr % 2
                            octr += 1

                            def s_mm(kt_, qb=qb, qs=qs):
                                pi_ = kt_ % 4
                                sbk = SBK[pi_]
                                P.op("pe", lambda e, sbk=sbk, kt_=kt_, qb=qb, qs=qs: e.matmul(ps[sbk][:], KR[:, kt_ * 128:(kt_ + 1) * 128], QR[qb][:, qs], start=True, stop=True),
                                     reads=[BKR, BQR[qb]], writes=[PB[sbk]])
                                P.op("act", lambda e, sbk=sbk, pi_=pi_: e.activation(out=pT[pi_][:], in_=ps[sbk][:], func=AF.Exp, scale=scale), reads=[PB[sbk]], writes=[PT[pi_]])

                            def pv_mm(kt_, ob=ob, db=db):
                                pi_ = kt_ % 4
                                P.op("pe", lambda e, pi_=pi_, kt_=kt_, ob=ob: e.matmul(ps[ob][:], vtok[:, kt_, :], pT[pi_][:], start=(kt_ == 0), stop=(kt_ == 15)),
                                     reads=[VT, PT[pi_]], writes=[PB[ob]])
                                P.op("pe", lambda e, pi_=pi_, kt_=kt_, db=db: e.matmul(ps[db][:], ones_b[:], pT[pi_][:], start=(kt_ == 0), stop=(kt_ == 15)),
                                     reads=[B_const, PT[pi_]], writes=[PB[db]])
                            for kt_ in range(3):
                                s_mm(kt_)
                            for kt_ in range(16):
                                if kt_ + 3 < 16:
                                    s_mm(kt_ + 3)
                                pv_mm(kt_)
                            P.op("dve", lambda e, o2=o2, db=db: e.reciprocal(out=rden[o2][:], in_=ps[db][:]), reads=[PB[db]], writes=[RD[o2]])
                            P.op("dve", lambda e, o2=o2, ob=ob: e.tensor_tensor(out=ost[o2][:], in0=ps[ob][:], in1=rden[o2][:], op=ALU.mult), reads=[PB[ob], RD[o2]], writes=[OS[o2]])
                            P.dma("sync", mixT[A_W + B_W + hq * 128:A_W + B_W + (hq + 1) * 128, qs], ost[o2][:], reads=[OS[o2]], writes=[B_mixT], key=f"at_o{o2}")
            P.barrier()

        if on("embed"):
            phase_embed()
        for l in layers:
            if on("win"):
                phase_win(l)
            if on("rglru"):
                phase_rglru(l)
            if on("hgrn"):
                phase_hgrn(l)
            if on("attn"):
                phase_attn(l)
            if on("wout"):
                phase_wout(l)
            if on("ln1"):
                phase_ln(f"ln1w{l}", f"ln1b{l}", False)
            if on("wup"):
                phase_wup(l)
            if on("wdown"):
                phase_wdown(l)
            if on("ln2"):
                phase_ln(f"ln2w{l}", f"ln2b{l}", l == DEPTH - 1 and phases is None)
        fw = [op for k, op in P.latest.items() if k.startswith("D_")]
        P.emit(final_waits=fw)
    return nc


_ROPE = None


def kernel(**inputs):
    global _ROPE
    n = 8
    nc = bass.Bass("TRN2", target_bir_lowering=False)
    build(nc)
    if _ROPE is None:
        _ROPE = _rope_tables()
    pp = _pack_params(inputs)
    f32 = lambda a: np.ascontiguousarray(np.asarray(a, np.float32))
    shared = {
        "emb_ln_w": f32(inputs["emb_ln_w"]), "emb_ln_b": f32(inputs["emb_ln_b"]),
        "w_in": f32(inputs["w_in"]), "w_out": f32(inputs["w_out"]),
        "ffn_w_up": f32(inputs["ffn_w_up"]), "ffn_w_down": f32(inputs["ffn_w_down"]),
        "rglru_wa": f32(inputs["rglru_wa"]), "rglru_wx": f32(inputs["rglru_wx"]),
        "pp": pp, "rope": _ROPE,
    }
    x = f32(inputs["x"])
    in_maps = [dict(shared, x=x[c]) for c in range(n)]
    res = run_bass_kernel_spmd(nc, in_maps, core_ids=list(range(n)))
    return np.stack([r["y"] for r in res.results], axis=0).astype(np.float32)
```

```python
import numpy as np
from contextlib import ExitStack
import concourse.bass as bass
import concourse.mybir as mybir
from concourse.bass_utils import run_bass_kernel_spmd

AF = mybir.ActivationFunctionType
ALU = mybir.AluOpType
DT = mybir.dt
F32 = DT.float32
BF16 = DT.bfloat16

SAME_ENG_SYNC = True

S = 2048
D = 4096
KT = D // 128
DEPTH = 2
A_W = 1024
B_W = 1024
C_W = 2048
A_COLS = 5 * A_W
B_COLS = 2 * B_W
IN_COLS = 10240
D_FF = 11008
FT = D_FF // 128
ALPHA = (2.0 * DEPTH) ** 0.25
LN_EPS = 1e-5
RMS_EPS = 1e-6


class Op:
    __slots__ = ("eng", "fn", "deps", "key", "count", "used", "idx")


class Buf:
    __slots__ = ("w", "r", "name", "excl")

    def __init__(self, name="", excl=False):
        self.w = None
        self.r = {}
        self.name = name
        self.excl = excl


def _semkey(op):
    return ("E_" + op.eng) if op.key is None else ("D_" + op.key)


class Prog:
    ENGS = ("sync", "act", "pool", "dve", "pe")

    def __init__(self, nc):
        self.nc = nc
        self.ops = {e: [] for e in self.ENGS}
        self.n = 0
        self.latest = {}
        self.pending = {e: None for e in self.ENGS}
        self.on_barrier = None

    def add(self, eng, fn, deps=(), key=None):
        op = Op()
        op.eng = eng
        op.fn = fn
        op.key = key
        op.count = None
        op.used = False
        dl = []
        pend = self.pending[eng]
        if pend is not None:
            deps = list(deps) + pend
            self.pending[eng] = None
        for d in deps:
            if d is None:
                continue
            if d.key is None and d.eng == eng and (eng == "pe" or not SAME_ENG_SYNC):
                continue
            d.used = True
            dl.append(d)
        op.deps = dl
        op.idx = self.n
        self.n += 1
        self.ops[eng].append(op)
        self.latest[_semkey(op)] = op
        return op

    def op(self, eng, fn, reads=(), writes=(), key=None, deps=()):
        dl = list(deps)
        ex = [b for b in reads if b.excl]
        if ex:
            reads = [b for b in reads if not b.excl]
            writes = list(writes) + ex
        for b in reads:
            if b.w is not None:
                dl.append(b.w)
        for b in writes:
            if b.w is not None:
                dl.append(b.w)
            dl.extend(b.r.values())
        o = self.add(eng, fn, dl, key)
        sk = _semkey(o)
        for b in reads:
            b.r[sk] = o
        for b in writes:
            b.w = o
            b.r = {}
        return o

    def dma(self, q, out, in_, reads=(), writes=(), key=None, deps=()):
        assert key is not None
        return self.op(q, lambda e: e.dma_start(out=out, in_=in_), reads, writes, key, deps)

    def barrier(self):
        if self.on_barrier is not None:
            self.on_barrier()
        last = list(self.latest.values())
        for e in self.ENGS:
            self.pending[e] = list(last)

    def emit(self, final_waits=()):
        nc = self.nc
        cnt = {}
        for e in self.ENGS:
            for op in self.ops[e]:
                if op.key is None and not op.used:
                    continue
                k = _semkey(op)
                cnt[k] = cnt.get(k, 0) + (1 if op.key is None else 16)
                op.count = cnt[k]
        with ExitStack() as st:
            sems = {k: st.enter_context(nc.semaphore("s_" + k)) for k in cnt}
            block = st.enter_context(nc.Block())

            def run_engine(ename, eng):
                waited = {}
                for op in self.ops[ename]:
                    need = {}
                    for d in op.deps:
                        k = _semkey(d)
                        if d.count > need.get(k, 0):
                            need[k] = d.count
                    for k, v in need.items():
                        if waited.get(k, 0) >= v:
                            continue
                        eng.wait_ge(sems[k], v)
                        waited[k] = v
                    ins = op.fn(eng)
                    if op.count is not None:
                        ins.then_inc(sems[_semkey(op)], 1 if op.key is None else 16)
                if ename == "sync":
                    need = {}
                    for d in final_waits:
                        k = _semkey(d)
                        need[k] = max(need.get(k, 0), d.count)
                    for k, v in need.items():
                        eng.wait_ge(sems[k], v)

            @block.sync
            def _(e):
                run_engine("sync", e)

            @block.scalar
            def _(e):
                run_engine("act", e)

            @block.gpsimd
            def _(e):
                run_engine("pool", e)

            @block.vector
            def _(e):
                run_engine("dve", e)

            @block.tensor
            def _(e):
                run_engine("pe", e)


def _pp_layout():
    off = {}
    n = 0

    def put(name, c):
        nonlocal n
        off[name] = n
        n += c
    put("lb_logits", DEPTH * 2 * 8)
    for l in range(DEPTH):
        put(f"conv_w{l}", 4 * 8)
        put(f"conv_b{l}", 8)
        put(f"ba{l}", 16)
        put(f"bx{l}", 16)
        put(f"lam{l}", 16)
        put(f"hnw{l}", 8)
        put(f"qnw{l}", 1)
        put(f"knw{l}", 1)
        put(f"ln1w{l}", 32)
        put(f"ln1b{l}", 32)
        put(f"ln2w{l}", 32)
        put(f"ln2b{l}", 32)
        put(f"fcw{l}", 3 * FT)
        put(f"fcb{l}", FT)
    return off, n


PP_OFF, PP_N = _pp_layout()


def _pack_params(inp):
    pp = np.zeros((128, PP_N), np.float32)

    def cols(name, arr):
        a = np.asarray(arr, np.float32)
        lead = int(np.prod(a.shape[:-1])) if a.ndim > 1 else 1
        a = a.reshape(lead, -1, 128)
        a = a.transpose(2, 0, 1).reshape(128, -1)
        pp[:, PP_OFF[name]:PP_OFF[name] + a.shape[1]] = a
    cols("lb_logits", inp["hgrn_lb_logits"])
    for l in range(DEPTH):
        cols(f"conv_w{l}", inp["rglru_conv_w"][l])
        cols(f"conv_b{l}", inp["rglru_conv_b"][l])
        cols(f"ba{l}", inp["rglru_ba"][l])
        cols(f"bx{l}", inp["rglru_bx"][l])
        cols(f"lam{l}", inp["rglru_lambda"][l])
        cols(f"hnw{l}", inp["hgrn_norm_w"][l])
        cols(f"qnw{l}", inp["attn_q_norm_w"][l])
        cols(f"knw{l}", inp["attn_k_norm_w"][l])
        cols(f"ln1w{l}", inp["ln1_w"][l])
        cols(f"ln1b{l}", inp["ln1_b"][l])
        cols(f"ln2w{l}", inp["ln2_w"][l])
        cols(f"ln2b{l}", inp["ln2_b"][l])
        cols(f"fcw{l}", inp["ffn_conv_w"][l])
        cols(f"fcb{l}", inp["ffn_conv_b"][l])
    return pp


def _rope_tables():
    rows = S // 64
    t = np.arange(S)
    row = (t // 64).astype(np.float32)
    col = (t % 64).astype(np.float32)
    inv_freq = (np.float32(10000.0) ** (-np.arange(0, 64, 2, dtype=np.float32) / np.float32(64))).astype(np.float32)
    ang_r = row[:, None] * inv_freq[None, :]
    ang_c = col[:, None] * inv_freq[None, :]
    ang = np.concatenate([ang_r, ang_r, ang_c, ang_c], axis=-1).astype(np.float32)
    return np.ascontiguousarray(np.stack([np.cos(ang).T, np.sin(ang).T]).astype(np.float32))


def rev_ap(ap2d):
    a = ap2d.ap
    n = ap2d.shape[1]
    return bass.AP(ap2d.tensor, ap2d.offset + (n - 1) * a[1][0], [list(a[0]), [-a[1][0], n]])


def bcast_last(ap3, n):
    a = ap3.ap
    return bass.AP(ap3.tensor, ap3.offset, [list(a[0]), list(a[1]), [0, n]])


def build(nc, phases=None, dump=(), inject=(), layers=(0, 1)):
    def on(name):
        return phases is None or name in phases

    def dram_in(name, shape, dt=F32):
        return nc.dram_tensor(name, list(shape), dt, kind="ExternalInput").ap()

    x = dram_in("x", [S, D])
    emb_w = dram_in("emb_ln_w", [D])
    emb_b = dram_in("emb_ln_b", [D])
    w_in = dram_in("w_in", [DEPTH, D, IN_COLS]) if on("win") else None
    w_out = dram_in("w_out", [DEPTH, D, D]) if on("wout") else None
    w_up = dram_in("ffn_w_up", [DEPTH, D, 2 * D_FF]) if on("wup") else None
    w_down = dram_in("ffn_w_down", [DEPTH, D_FF, D]) if on("wdown") else None
    wa_d = dram_in("rglru_wa", [DEPTH, 2, 8, 128, 128])
    wx_d = dram_in("rglru_wx", [DEPTH, 2, 8, 128, 128])
    pp_d = dram_in("pp", [128, PP_N])
    rope_d = dram_in("rope", [2, 128, S])
    y_out = nc.dram_tensor("y", [S, D], F32, kind="ExternalOutput").ap()

    def scratch(name, shape, dt):
        kind = "Internal"
        if name in dump:
            kind = "ExternalOutput"
        if name in inject:
            kind = "ExternalInput"
        return nc.dram_tensor(name, list(shape), dt, kind=kind).ap()

    hT_f = scratch("hT_f", [D, S], F32)
    hT_b = scratch("hT_b", [D, S], BF16)
    uT = scratch("uT", [IN_COLS, S], F32)
    mixT = scratch("mixT", [D, S], BF16)
    zT = scratch("zT", [D, S], F32)
    gT = scratch("gT", [D_FF, S], BF16)
    wdT = scratch("wdT", [D // 128, 128, FT * 128], BF16)
    B_wd = Buf("wdT")
    B_hT_f, B_hT_b, B_uT, B_mixT, B_zT, B_gT = (Buf(n) for n in ("hT_f", "hT_b", "uT", "mixT", "zT", "gT"))

    P = Prog(nc)
    final = []
    with ExitStack() as glob:
        def gsb(name, shape, dt):
            return glob.enter_context(nc.sbuf_tensor(name, shape, dt))
        uid = [0]

        def _bump():
            uid[0] += 1
        P.on_barrier = _bump

        def sbt(name, shape, dt):
            return nc.sbuf_tensor(f"{name}_u{uid[0]}", shape, dt)
        ps = [glob.enter_context(nc.psum_tensor(f"ps{i}", [128, 512], F32)) for i in range(8)]
        PB = [Buf(f"ps{i}", excl=True) for i in range(8)]
        ident = gsb("ident", [128, 128], F32)
        ones_f = gsb("ones_f", [128, 128], F32)
        ones_b = gsb("ones_b", [128, 128], BF16)
        pp = gsb("pp_sb", [128, PP_N], F32)
        B_const = Buf("const")

        def pc(name, i=0):
            o = PP_OFF[name] + i
            return pp[:, o:o + 1]

        P.dma("sync", pp[:], pp_d, writes=[B_const], key="pp")
        P.op("pool", lambda e: e.memset(ident[:], 0.0), writes=[B_const])
        P.op("pool", lambda e: e.affine_select(out=ident[:], in_=ident[:], pattern=[[-1, 128]], compare_op=ALU.not_equal,
                                               fill=1.0, base=0, channel_multiplier=1), writes=[B_const])
        P.op("pool", lambda e: e.memset(ones_f[:], 1.0), writes=[B_const])
        P.op("pool", lambda e: e.memset(ones_b[:], 1.0), writes=[B_const])
        P.barrier()

        def phase_embed():
            with ExitStack() as ph:
                def sb(name, shape, dt):
                    return ph.enter_context(sbt(name, shape, dt))
                xt = [sb(f"e_xt{i}", [128, D], F32) for i in range(2)]
                XT = [Buf() for _ in range(2)]
                wB = sb("e_wB", [128, D], F32)
                bB = sb("e_bB", [128, D], F32)
                WB = Buf()
                stg_f = [sb(f"e_sf{i}", [128, KT, 128], F32) for i in range(2)]
                stg_b = [sb(f"e_sb{i}", [128, KT, 128], BF16) for i in range(2)]
                SF = [Buf() for _ in range(2)]
                SBb = [Buf() for _ in range(2)]
                stats = sb("e_stats", [128, 8, 6], F32)
                mv = sb("e_mv", [128, 2], F32)
                rstd = sb("e_rstd", [128, 1], F32)
                ST = Buf()
                P.dma("sync", wB[:], emb_w.partition_broadcast(128), writes=[WB], key="e_wB")
                P.dma("sync", bB[:], emb_b.partition_broadcast(128), writes=[WB], key="e_bB")
                hTf_v = hT_f.rearrange("(kt p) n -> p kt n", p=128)
                hTb_v = hT_b.rearrange("(kt p) n -> p kt n", p=128)
                bank = 0
                import os
                for t in range(int(os.environ.get("EMB_TILES", S // 128))):
                    b = t % 2
                    P.dma("sync", xt[b][:], x[t * 128:(t + 1) * 128, :], writes=[XT[b]], key=f"e_xt{b}")
                    for c in range(8):
                        P.op("dve", lambda e, b=b, c=c: e.bn_stats(out=stats[:, c, :], in_=xt[b][:, c * 512:(c + 1) * 512]),
                             reads=[XT[b]], writes=[ST])
                    P.op("dve", lambda e: e.bn_aggr(out=mv[:], in_=stats[:]), writes=[ST])
                    P.op("act", lambda e: e.activation(out=rstd[:], in_=mv[:, 1:2], func=AF.Sqrt, bias=LN_EPS, scale=1.0), writes=[ST])
                    P.op("dve", lambda e: e.reciprocal(out=rstd[:], in_=rstd[:]), writes=[ST])
                    P.op("dve", lambda e, b=b: e.tensor_scalar(out=xt[b][:], in0=xt[b][:], scalar1=mv[:, 0:1], scalar2=rstd[:, 0:1],
                                                               op0=ALU.subtract, op1=ALU.mult), reads=[ST], writes=[XT[b]])
                    P.op("dve", lambda e, b=b: e.tensor_tensor(out=xt[b][:], in0=xt[b][:], in1=wB[:], op=ALU.mult), reads=[WB], writes=[XT[b]])
                    P.op("pool", lambda e, b=b: e.tensor_tensor(out=xt[b][:], in0=xt[b][:], in1=bB[:], op=ALU.add), reads=[WB], writes=[XT[b]])
                    for g in range(8):
                        bk = bank % 8
                        bank += 1
                        for i in range(4):
                            k = g * 4 + i
                            P.op("pe", lambda e, bk=bk, i=i, k=k, b=b: e.transpose(ps[bk][:, i * 128:(i + 1) * 128], xt[b][:, k * 128:(k + 1) * 128], ident[:]),
                                 reads=[XT[b], B_const], writes=[PB[bk]])
                        P.op("act", lambda e, bk=bk, g=g, b=b: e.copy(out=stg_f[b][:, g * 4:(g + 1) * 4, :], in_=ps[bk][:].rearrange("p (a n) -> p a n", a=4)),
                             reads=[PB[bk]], writes=[SF[b]])
                        if not os.environ.get("EMB_NOBF"):
                            P.op("dve", lambda e, bk=bk, g=g, b=b: e.tensor_copy(out=stg_b[b][:, g * 4:(g + 1) * 4, :], in_=ps[bk][:].rearrange("p (a n) -> p a n", a=4)),
                                 reads=[PB[bk]], writes=[SBb[b]])
                    P.dma("act", hTf_v[:, :, t * 128:(t + 1) * 128], stg_f[b][:], reads=[SF[b]], writes=[B_hT_f], key=f"e_sf{b}")
                    if not os.environ.get("EMB_NOBF"):
                        P.dma(os.environ.get("EMB_Q", "act"), hTb_v[:, :, t * 128:(t + 1) * 128], stg_b[b][:], reads=[SBb[b]], writes=[B_hT_b], key=f"e_sb{b}")
            P.barrier()

        def load_act(ph, name, src, kt, tok0, ntok, Bsrc):
            t = ph.enter_context(sbt(name, [128, kt, ntok], BF16))
            Bt = Buf()
            v = src.rearrange("(kt p) n -> p kt n", p=128)
            nsp = 4
            step = (kt + nsp - 1) // nsp
            for i in range(nsp):
                k0, k1 = i * step, min(kt, (i + 1) * step)
                if k0 >= k1:
                    continue
                P.dma("sync" if i % 2 == 0 else "act", t[:, k0:k1, :], v[:, k0:k1, tok0:tok0 + ntok], reads=[Bsrc], writes=[Bt], key=f"{name}_{i}")
            return t, Bt

        def mm_stream(ph, tag, actT, Bact, kt, ntok, wsrc, col_tiles, evac, cw, nwb):
            wv = wsrc.rearrange("(kt p) n -> p kt n", p=128)
            wbuf = [ph.enter_context(sbt(f"{tag}_w{i}", [128, kt, cw], BF16)) for i in range(nwb)]
            WBs = [Buf() for _ in range(nwb)]
            nj = ntok // 512
            per = cw // 128
            nblk = len(col_tiles) // per
            bankctr = 0
            for blk in range(nblk):
                wb = wbuf[blk % nwb]
                Bw = WBs[blk % nwb]
                c_first = col_tiles[blk * per]
                P.dma("pool", wb[:], wv[:, :, c_first:c_first + cw], writes=[Bw], key=f"{tag}_w{blk % nwb}")
                for ct in range(per):
                    ci = blk * per + ct
                    banks = [(bankctr + j) % 8 for j in range(nj)]
                    bankctr += nj
                    for k in range(kt):
                        for j in range(nj):
                            P.op("pe", lambda e, bk=banks[j], k=k, ct=ct, j=j, wb=wb: e.matmul(
                                ps[bk][:], wb[:, k, ct * 128:(ct + 1) * 128], actT[:, k, j * 512:(j + 1) * 512],
                                start=(k == 0), stop=(k == kt - 1)), reads=[Bw, Bact], writes=[PB[banks[j]]])
                    evac(ci, col_tiles[ci], banks)

        def phase_win(l):
            with ExitStack() as ph:
                actT, Bact = load_act(ph, "wi_act", hT_b, KT, 0, S, B_hT_b)
                stg = [ph.enter_context(sbt(f"wi_stg{i}", [128, S], F32)) for i in range(2)]
                SG = [Buf() for _ in range(2)]

                def evac(ci, c0, banks):
                    s = ci % 2
                    for j, bk in enumerate(banks):
                        if j % 2 == 0:
                            P.op("act", lambda e, bk=bk, j=j, s=s: e.copy(out=stg[s][:, j * 512:(j + 1) * 512], in_=ps[bk][:]), reads=[PB[bk]], writes=[SG[s]])
                        else:
                            P.op("dve", lambda e, bk=bk, j=j, s=s: e.tensor_copy(out=stg[s][:, j * 512:(j + 1) * 512], in_=ps[bk][:]), reads=[PB[bk]], writes=[SG[s]])
                    P.dma("sync", uT[c0:c0 + 128, :], stg[s][:], reads=[SG[s]], writes=[B_uT], key=f"wi_stg{s}")
                mm_stream(ph, "wi", actT, Bact, KT, S, w_in[l], [c * 128 for c in range(IN_COLS // 128)], evac, 256, 3)
            P.barrier()

        def phase_wout(l):
            with ExitStack() as ph:
                actT, Bact = load_act(ph, "wo_act", mixT, KT, 0, S, B_mixT)
                stg = [ph.enter_context(sbt(f"wo_stg{i}", [128, S], F32)) for i in range(2)]
                hres = [ph.enter_context(sbt(f"wo_hr{i}", [128, S], F32)) for i in range(2)]
                SG = [Buf() for _ in range(2)]
                HR = [Buf() for _ in range(2)]

                def evac(ci, c0, banks):
                    s = ci % 2
                    P.dma("sync", hres[s][:], hT_f[c0:c0 + 128, :], reads=[B_hT_f], writes=[HR[s]], key=f"wo_hr{s}")
                    for j, bk in enumerate(banks):
                        P.op("dve", lambda e, bk=bk, j=j, s=s: e.scalar_tensor_tensor(
                            out=stg[s][:, j * 512:(j + 1) * 512], in0=hres[s][:, j * 512:(j + 1) * 512], scalar=float(ALPHA),
                            in1=ps[bk][:], op0=ALU.mult, op1=ALU.add), reads=[PB[bk], HR[s]], writes=[SG[s]])
                    P.dma("sync", zT[c0:c0 + 128, :], stg[s][:], reads=[SG[s]], writes=[B_zT], key=f"wo_stg{s}")
                mm_stream(ph, "wo", actT, Bact, KT, S, w_out[l], [c * 128 for c in range(D // 128)], evac, 256, 2)
            P.barrier()

        def phase_ln(wname, bname, last):
            with ExitStack() as ph:
                def sb(name, shape, dt):
                    return ph.enter_context(sbt(name, shape, dt))
                zt = [sb(f"ln_z{i}", [128, S], F32) for i in range(3)]
                ZT = [Buf() for _ in range(3)]
                sq = [sb(f"ln_sq{i}", [128, S], F32) for i in range(2)]
                SQ = [Buf() for _ in range(2)]
                mean = sb("ln_mean", [128, S], F32)
                rstd = sb("ln_rstd", [128, S], F32)
                MS = Buf()
                t1 = [sb(f"ln_t1{i}", [128, S], F32) for i in range(2)]
                T1 = [Buf() for _ in range(2)]
                hb = [sb(f"ln_hb{i}", [128, S], BF16) for i in range(2)]
                HB = [Buf() for _ in range(2)]
                ystg = [sb(f"ln_y{i}", [128, 16, 128], F32) for i in range(2)] if last else None
                YS = [Buf() for _ in range(2)]
                for k in range(KT):
                    b = k % 3
                    P.dma("sync", zt[b][:], zT[k * 128:(k + 1) * 128, :], reads=[B_zT], writes=[ZT[b]], key=f"ln_z{b}")
                    P.op("act", lambda e, b=b, k=k: e.activation(out=sq[k % 2][:], in_=zt[b][:], func=AF.Square), reads=[ZT[b]], writes=[SQ[k % 2]])
                    for j in range(4):
                        P.op("pe", lambda e, j=j, b=b, k=k: e.matmul(ps[j][:], ones_f[:], zt[b][:, j * 512:(j + 1) * 512], start=(k == 0), stop=(k == KT - 1)),
                             reads=[ZT[b], B_const], writes=[PB[j]])
                    for j in range(4):
                        P.op("pe", lambda e, j=j, k=k: e.matmul(ps[4 + j][:], ones_f[:], sq[k % 2][:, j * 512:(j + 1) * 512], start=(k == 0), stop=(k == KT - 1)),
                             reads=[SQ[k % 2], B_const], writes=[PB[4 + j]])
                for j in range(4):
                    sl = slice(j * 512, (j + 1) * 512)
                    P.op("act", lambda e, j=j, sl=sl: e.mul(out=mean[:, sl], in_=ps[j][:], mul=1.0 / D), reads=[PB[j]], writes=[MS])
                    P.op("act", lambda e, j=j, sl=sl: e.mul(out=rstd[:, sl], in_=ps[4 + j][:], mul=1.0 / D), reads=[PB[4 + j]], writes=[MS])
                P.op("dve", lambda e: e.tensor_tensor(out=t1[0][:], in0=mean[:], in1=mean[:], op=ALU.mult), reads=[MS], writes=[T1[0]])
                P.op("dve", lambda e: e.tensor_tensor(out=rstd[:], in0=rstd[:], in1=t1[0][:], op=ALU.subtract), reads=[T1[0]], writes=[MS])
                P.op("act", lambda e: e.activation(out=rstd[:], in_=rstd[:], func=AF.Sqrt, bias=LN_EPS, scale=1.0), writes=[MS])
                P.op("dve", lambda e: e.reciprocal(out=rstd[:], in_=rstd[:]), writes=[MS])
                bank = 0
                yv = y_out.rearrange("(t p) f -> p t f", p=128)
                for k in range(KT):
                    b = k % 3
                    s = k % 2
                    P.dma("sync", zt[b][:], zT[k * 128:(k + 1) * 128, :], reads=[B_zT], writes=[ZT[b]], key=f"ln_z{b}")
                    P.op("dve", lambda e, b=b, s=s: e.tensor_tensor(out=t1[s][:], in0=zt[b][:], in1=mean[:], op=ALU.subtract), reads=[ZT[b], MS], writes=[T1[s]])
                    P.op("pool", lambda e, s=s: e.tensor_tensor(out=t1[s][:], in0=t1[s][:], in1=rstd[:], op=ALU.mult), reads=[MS], writes=[T1[s]])
                    P.op("act", lambda e, s=s, k=k: e.activation(out=t1[s][:], in_=t1[s][:], func=AF.Identity, bias=pc(bname, k), scale=pc(wname, k)),
                         reads=[B_const], writes=[T1[s]])
                    if not last:
                        P.op("act", lambda e, s=s: e.copy(out=hb[s][:], in_=t1[s][:]), reads=[T1[s]], writes=[HB[s]])
                        P.dma("act", hT_f[k * 128:(k + 1) * 128, :], t1[s][:], reads=[T1[s]], writes=[B_hT_f], key=f"ln_t1{s}")
                        P.dma("act", hT_b[k * 128:(k + 1) * 128, :], hb[s][:], reads=[HB[s]], writes=[B_hT_b], key=f"ln_hb{s}")
                    else:
                        for g in range(4):
                            bk = bank % 8
                            bank += 1
                            for i in range(4):
                                tt = g * 4 + i
                                P.op("pe", lambda e, bk=bk, i=i, tt=tt, s=s: e.transpose(ps[bk][:, i * 128:(i + 1) * 128], t1[s][:, tt * 128:(tt + 1) * 128], ident[:]),
                                     reads=[T1[s], B_const], writes=[PB[bk]])
                            eng = "act" if g % 2 == 0 else "dve"
                            if eng == "act":
                                P.op("act", lambda e, bk=bk, g=g, s=s: e.copy(out=ystg[s][:, g * 4:(g + 1) * 4, :], in_=ps[bk][:].rearrange("p (a n) -> p a n", a=4)),
                                     reads=[PB[bk]], writes=[YS[s]])
                            else:
                                P.op("dve", lambda e, bk=bk, g=g, s=s: e.tensor_copy(out=ystg[s][:, g * 4:(g + 1) * 4, :], in_=ps[bk][:].rearrange("p (a n) -> p a n", a=4)),
                                     reads=[PB[bk]], writes=[YS[s]])
                        d = P.dma("act", yv[:, :, k * 128:(k + 1) * 128], ystg[s][:], reads=[YS[s]], key=f"ln_y{s}")
                        final.append(d)
            P.barrier()

        def phase_wup(l):
            with ExitStack() as ph:
                def sb(name, shape, dt):
                    return ph.enter_context(sbt(name, shape, dt))
                actT, Bact = load_act(ph, "wu_act", hT_b, KT, 0, S, B_hT_b)
                wg = [sb(f"wu_wg{i}", [128, KT, 128], BF16) for i in range(2)]
                wu = [sb(f"wu_wu{i}", [128, KT, 128], BF16) for i in range(2)]
                WG = [Buf() for _ in range(2)]
                WU = [Buf() for _ in range(2)]
                gsbuf = sb("wu_gs", [128, S + 2], F32)
                GS = Buf()
                cv = sb("wu_cv", [128, S], F32)
                CV = Buf()
                ob = [sb(f"wu_ob{i}", [128, S], BF16) for i in range(2)]
                OB = [Buf() for _ in range(2)]
                P.op("pool", lambda e: e.memset(gsbuf[:, 0:1], 0.0), writes=[GS])
                P.op("pool", lambda e: e.memset(gsbuf[:, S + 1:S + 2], 0.0), writes=[GS])
                wv = w_up[l].rearrange("(kt p) n -> p kt n", p=128)
                for m in range(FT):
                    b = m % 2
                    P.dma("pool", wg[b][:], wv[:, :, m * 128:(m + 1) * 128], writes=[WG[b]], key=f"wu_wg{b}")
                    P.dma("pool", wu[b][:], wv[:, :, D_FF + m * 128:D_FF + (m + 1) * 128], writes=[WU[b]], key=f"wu_wu{b}")
                    for k in range(KT):
                        for j in range(4):
                            P.op("pe", lambda e, j=j, k=k, b=b: e.matmul(ps[j][:], wg[b][:, k, :], actT[:, k, j * 512:(j + 1) * 512], start=(k == 0), stop=(k == KT - 1)),
                                 reads=[WG[b], Bact], writes=[PB[j]])
                    for j in range(4):
                        P.op("act", lambda e, j=j: e.copy(out=gsbuf[:, 1 + j * 512:1 + (j + 1) * 512], in_=ps[j][:]), reads=[PB[j]], writes=[GS])
                    for k in range(KT):
                        for j in range(4):
                            P.op("pe", lambda e, j=j, k=k, b=b: e.matmul(ps[4 + j][:], wu[b][:, k, :], actT[:, k, j * 512:(j + 1) * 512], start=(k == 0), stop=(k == KT - 1)),
                                 reads=[WU[b], Bact], writes=[PB[4 + j]])
                    P.op("dve", lambda e, m=m: e.tensor_scalar(out=cv[:], in0=gsbuf[:, 1:S + 1], scalar1=pc(f"fcw{l}", FT + m), scalar2=pc(f"fcb{l}", m),
                                                               op0=ALU.mult, op1=ALU.add), reads=[GS, B_const], writes=[CV])
                    P.op("dve", lambda e, m=m: e.scalar_tensor_tensor(out=cv[:], in0=gsbuf[:, 0:S], scalar=pc(f"fcw{l}", m), in1=cv[:],
                                                                      op0=ALU.mult, op1=ALU.add), reads=[GS, B_const], writes=[CV])
                    P.op("dve", lambda e, m=m: e.scalar_tensor_tensor(out=cv[:], in0=gsbuf[:, 2:S + 2], scalar=pc(f"fcw{l}", 2 * FT + m), in1=cv[:],
                                                                      op0=ALU.mult, op1=ALU.add), reads=[GS, B_const], writes=[CV])
                    P.op("act", lambda e: e.activation(out=cv[:], in_=cv[:], func=AF.Silu), writes=[CV])
                    for j in range(4):
                        P.op("dve", lambda e, j=j, b=b: e.tensor_tensor(out=ob[b][:, j * 512:(j + 1) * 512], in0=cv[:, j * 512:(j + 1) * 512], in1=ps[4 + j][:], op=ALU.mult),
                             reads=[CV, PB[4 + j]], writes=[OB[b]])
                    P.dma("sync", gT[m * 128:(m + 1) * 128, :], ob[b][:], reads=[OB[b]], writes=[B_gT], key=f"wu_ob{b}")
            P.barrier()

        def phase_wdown(l):
            with ExitStack() as ph:
                def sb(name, shape, dt):
                    return ph.enter_context(sbt(name, shape, dt))
                TB = 512
                gblk = sb("wd_g", [128, FT, TB], BF16)
                GB = Buf()
                wbuf = [sb(f"wd_w{i}", [128, FT, 128], BF16) for i in range(3)]
                WBs = [Buf() for _ in range(3)]
                hres = [sb(f"wd_hr{i}", [128, TB], F32) for i in range(2)]
                HR = [Buf() for _ in range(2)]
                stg = [sb(f"wd_stg{i}", [128, TB], F32) for i in range(2)]
                SG = [Buf() for _ in range(2)]
                gv = gT.rearrange("(kt p) n -> p kt n", p=128)
                wv = w_down[l].rearrange("(kt p) n -> p kt n", p=128)
                ctr = 0
                for tb in range(S // TB):
                    t0 = tb * TB
                    for i, (k0, k1) in enumerate(((0, 22), (22, 44), (44, 66), (66, FT))):
                        P.dma("sync", gblk[:, k0:k1, :], gv[:, k0:k1, t0:t0 + TB], reads=[B_gT], writes=[GB], key=f"wd_g{i}")
                    for c in range(D // 128):
                        wb = ctr % 3
                        bk = ctr % 8
                        s = ctr % 2
                        ctr += 1
                        P.dma("act", wbuf[wb][:].rearrange("p k n -> p (k n)"), wdT[c], reads=[B_wd], writes=[WBs[wb]], key=f"wd_w{wb}")
                        P.dma("sync", hres[s][:], hT_f[c * 128:(c + 1) * 128, t0:t0 + TB], reads=[B_hT_f], writes=[HR[s]], key=f"wd_hr{s}")
                        for k in range(FT):
                            P.op("pe", lambda e, bk=bk, k=k, wb=wb: e.matmul(ps[bk][:], wbuf[wb][:, k, :], gblk[:, k, :], start=(k == 0), stop=(k == FT - 1)),
                                 reads=[WBs[wb], GB], writes=[PB[bk]])
                        P.op("dve", lambda e, bk=bk, s=s: e.scalar_tensor_tensor(out=stg[s][:], in0=hres[s][:], scalar=float(ALPHA), in1=ps[bk][:],
                                                                                 op0=ALU.mult, op1=ALU.add), reads=[PB[bk], HR[s]], writes=[SG[s]])
                        P.dma("sync", zT[c * 128:(c + 1) * 128, t0:t0 + TB], stg[s][:], reads=[SG[s]], writes=[B_zT], key=f"wd_stg{s}")
            P.barrier()

        def wd_convert(sb, tag, l, c0, c1):
            cvt = [sb(f"{tag}_cv{i}", [128, FT, 128], BF16) for i in range(2)]
            CVB = [Buf() for _ in range(2)]
            wdv = w_down[l].rearrange("(kt p) n -> p kt n", p=128)
            for c in range(c0, c1):
                b = c % 2
                P.dma("pool", cvt[b][:], wdv[:, :, c * 128:(c + 1) * 128], writes=[CVB[b]], key=f"{tag}_cv{b}")
                P.dma("pool", wdT[c], cvt[b][:].rearrange("p k n -> p (k n)"), reads=[CVB[b]], writes=[B_wd], key=f"{tag}_cvs{b}")

        def phase_rglru(l):
            with ExitStack() as ph:
                def sb(name, shape, dt=F32):
                    return ph.enter_context(sbt(name, shape, dt))
                xb = [sb(f"rg_xb{i}", [128, S]) for i in range(2)]
                gt = [sb(f"rg_gt{i}", [128, S]) for i in range(2)]
                XB = [Buf() for _ in range(2)]
                GT = [Buf() for _ in range(2)]
                names = ["XC", "R", "IG", "A", "M", "BT", "HS0", "HS1", "G2"]
                T = {n: sb("rg_" + n, [128, S]) for n in names}
                Bf = {n: Buf() for n in names}
                yb = [sb(f"rg_y{i}", [128, S], BF16) for i in range(2)]
                YB = [Buf() for _ in range(2)]
                wa = sb("rg_wa", [128, 16, 128])
                wx = sb("rg_wx", [128, 16, 128])
                cdp = sb("rg_cd", [128, 16])
                PR = Buf()
                P.dma("sync", wa[:], wa_d[l].rearrange("r n d e -> d (r n) e"), writes=[PR], key="rg_wa")
                P.dma("sync", wx[:], wx_d[l].rearrange("r n d e -> d (r n) e"), writes=[PR], key="rg_wx")
                lo = PP_OFF[f"lam{l}"]
                P.op("act", lambda e: e.activation(out=cdp[:], in_=pp[:, lo:lo + 16], func=AF.Exp, scale=-1.0), reads=[B_const], writes=[PR])
                P.op("act", lambda e: e.activation(out=cdp[:], in_=cdp[:], func=AF.Ln, bias=1.0), writes=[PR])
                P.op("dve", lambda e: e.tensor_scalar(out=cdp[:], in0=cdp[:], scalar1=-8.0, scalar2=None, op0=ALU.mult), writes=[PR])
                if w_down is not None:
                    wd_convert(sb, "rg", l, 0, 16)
                r0 = A_COLS
                for n in range(8):
                    b = n % 2
                    P.dma("sync", xb[b][:], uT[r0 + n * 128:r0 + (n + 1) * 128, :], reads=[B_uT], writes=[XB[b]], key=f"rg_xb{b}")
                    P.dma("sync", gt[b][:], uT[r0 + B_W + n * 128:r0 + B_W + (n + 1) * 128, :], reads=[B_uT], writes=[GT[b]], key=f"rg_gt{b}")
                    XC = T["XC"]
                    cws = [pc(f"conv_w{l}", k * 8 + n) for k in range(4)]
                    cbn = pc(f"conv_b{l}", n)
                    P.op("dve", lambda e, b=b, w2=cws[2], cbn=cbn: e.tensor_scalar(out=XC[:], in0=xb[b][:], scalar1=w2, scalar2=cbn, op0=ALU.mult, op1=ALU.add),
                         reads=[XB[b], B_const], writes=[Bf["XC"]])
                    P.op("dve", lambda e, b=b, w=cws[0]: e.scalar_tensor_tensor(out=XC[:, 2:S], in0=xb[b][:, 0:S - 2], scalar=w, in1=XC[:, 2:S], op0=ALU.mult, op1=ALU.add),
                         reads=[XB[b]], writes=[Bf["XC"]])
                    P.op("dve", lambda e, b=b, w=cws[1]: e.scalar_tensor_tensor(out=XC[:, 1:S], in0=xb[b][:, 0:S - 1], scalar=w, in1=XC[:, 1:S], op0=ALU.mult, op1=ALU.add),
                         reads=[XB[b]], writes=[Bf["XC"]])
                    P.op("dve", lambda e, b=b, w=cws[3]: e.scalar_tensor_tensor(out=XC[:, 0:S - 1], in0=xb[b][:, 1:S], scalar=w, in1=XC[:, 0:S - 1], op0=ALU.mult, op1=ALU.add),
                         reads=[XB[b]], writes=[Bf["XC"]])
                    for d in range(2):
                        HS = T[f"HS{d}"]
                        for j in range(4):
                            P.op("pe", lambda e, j=j, d=d, n=n: e.matmul(ps[j][:], wa[:, d * 8 + n, :], XC[:, j * 512:(j + 1) * 512], start=True, stop=True),
                                 reads=[Bf["XC"], PR], writes=[PB[j]])
                        for j in range(4):
                            P.op("pe", lambda e, j=j, d=d, n=n: e.matmul(ps[4 + j][:], wx[:, d * 8 + n, :], XC[:, j * 512:(j + 1) * 512], start=True, stop=True),
                                 reads=[Bf["XC"], PR], writes=[PB[4 + j]])
                        for j in range(4):
                            P.op("act", lambda e, j=j, d=d, n=n: e.activation(out=T["R"][:, j * 512:(j + 1) * 512], in_=ps[j][:], func=AF.Sigmoid, bias=pc(f"ba{l}", d * 8 + n)),
                                 reads=[PB[j], B_const], writes=[Bf["R"]])
                        for j in range(4):
                            P.op("act", lambda e, j=j, d=d, n=n: e.activation(out=T["IG"][:, j * 512:(j + 1) * 512], in_=ps[4 + j][:], func=AF.Sigmoid, bias=pc(f"bx{l}", d * 8 + n)),
                                 reads=[PB[4 + j], B_const], writes=[Bf["IG"]])
                        P.op("act", lambda e, d=d, n=n: e.activation(out=T["A"][:], in_=T["R"][:], func=AF.Exp, scale=cdp[:, d * 8 + n:d * 8 + n + 1]),
                             reads=[Bf["R"], PR], writes=[Bf["A"]])
                        P.op("act", lambda e: e.activation(out=T["M"][:], in_=T["A"][:], func=AF.Square), reads=[Bf["A"]], writes=[Bf["M"]])
                        P.op("dve", lambda e: e.tensor_scalar(out=T["M"][:], in0=T["M"][:], scalar1=-1.0, scalar2=1.0, op0=ALU.mult, op1=ALU.add), writes=[Bf["M"]])
                        P.op("act", lambda e: e.activation(out=T["M"][:], in_=T["M"][:], func=AF.Sqrt), writes=[Bf["M"]])
                        P.op("dve", lambda e: e.tensor_tensor(out=T["BT"][:], in0=T["IG"][:], in1=XC[:], op=ALU.mult), reads=[Bf["IG"], Bf["XC"]], writes=[Bf["BT"]])
                        P.op("dve", lambda e: e.tensor_tensor(out=T["BT"][:], in0=T["BT"][:], in1=T["M"][:], op=ALU.mult), reads=[Bf["M"]], writes=[Bf["BT"]])
                        if d == 0:
                            P.op("dve", lambda e, HS=HS: e.tensor_tensor_scan(out=HS[:], data0=T["A"][:], data1=T["BT"][:], initial=0.0, op0=ALU.mult, op1=ALU.add),
                                 reads=[Bf["A"], Bf["BT"]], writes=[Bf["HS0"]])
                        else:
                            P.op("dve", lambda e, HS=HS: e.tensor_tensor_scan(out=rev_ap(HS[:]), data0=rev_ap(T["A"][:]), data1=rev_ap(T["BT"][:]), initial=0.0,
                                                                              op0=ALU.mult, op1=ALU.add), reads=[Bf["A"], Bf["BT"]], writes=[Bf["HS1"]])
                    P.op("dve", lambda e: e.tensor_tensor(out=T["HS0"][:], in0=T["HS0"][:], in1=T["HS1"][:], op=ALU.add), reads=[Bf["HS1"]], writes=[Bf["HS0"]])
                    G2 = T["G2"]
                    P.op("act", lambda e, b=b: e.activation(out=G2[:], in_=gt[b][:], func=AF.Square), reads=[GT[b]], writes=[Bf["G2"]])
                    P.op("dve", lambda e: e.tensor_scalar(out=G2[:], in0=G2[:], scalar1=0.044715, scalar2=1.0, op0=ALU.mult, op1=ALU.add), writes=[Bf["G2"]])
                    P.op("dve", lambda e, b=b: e.tensor_tensor(out=G2[:], in0=G2[:], in1=gt[b][:], op=ALU.mult), reads=[GT[b]], writes=[Bf["G2"]])
                    P.op("act", lambda e: e.activation(out=G2[:], in_=G2[:], func=AF.Sigmoid, scale=1.5957691216057308), writes=[Bf["G2"]])
                    P.op("dve", lambda e, b=b: e.tensor_tensor(out=G2[:], in0=G2[:], in1=gt[b][:], op=ALU.mult), reads=[GT[b]], writes=[Bf["G2"]])
                    P.op("dve", lambda e, b=b: e.tensor_tensor(out=yb[b][:], in0=G2[:], in1=T["HS0"][:], op=ALU.mult), reads=[Bf["G2"], Bf["HS0"]], writes=[YB[b]])
                    P.dma("act", mixT[A_W + n * 128:A_W + (n + 1) * 128, :], yb[b][:], reads=[YB[b]], writes=[B_mixT], key=f"rg_y{b}")
            P.barrier()

        def phase_hgrn(l):
            with ExitStack() as ph:
                def sb(name, shape, dt=F32):
                    return ph.enter_context(sbt(name, shape, dt))
                inn = ["q", "zf", "zb", "iv", "g"]
                IN = [{n: sb(f"hg_{n}{i}", [128, S]) for n in inn} for i in range(2)]
                BIN = [{n: Buf() for n in inn} for i in range(2)]
                names = ["F", "KK", "LF", "B", "E"]
                T = {n: sb("hg_" + n, [128, S]) for n in names}
                Bf = {n: Buf() for n in names}
                vtok = sb("hg_vtok", [128, 16, 128], BF16)
                khtok = sb("hg_khtok", [128, 16, 128], BF16)
                VT, KH = Buf(), Buf()
                Eb = sb("hg_Eb", [128, S], BF16)
                Fb = sb("hg_Fb", [128, S], BF16)
                BEb, BFb = Buf(), Buf()
                Sb = [sb(f"hg_Sb{i}", [128, 128], BF16) for i in range(4)]
                SSb = [Buf() for _ in range(4)]
                oacc = sb("hg_oacc", [128, S])
                OA = Buf()
                Sst = [sb(f"hg_S{i}", [128, 128]) for i in range(2)]
                SS = [Buf() for _ in range(2)]
                stm = [sb(f"hg_stm{i}", [128, 128], BF16) for i in range(2)]
                STM = [Buf() for _ in range(2)]
                dec = sb("hg_dec", [128, 32])
                DEC = Buf()
                mk = [sb(f"hg_mk{i}", [128, 128]) for i in range(2)]
                smk = [sb(f"hg_smk{i}", [128, S]) for i in range(2)]
                lbp = sb("hg_lb", [128, 16])
                oml = sb("hg_oml", [128, 16])
                e01 = sb("hg_e01", [128, 32])
                MK = Buf()
                yb = [sb(f"hg_y{i}", [128, S], BF16) for i in range(2)]
                YB = [Buf() for _ in range(2)]
                P.op("pool", lambda e: e.memset(mk[0][:], 1.0), writes=[MK])
                P.op("pool", lambda e: e.affine_select(out=mk[0][:], in_=mk[0][:], pattern=[[1, 128]], compare_op=ALU.is_ge, fill=0.0, base=0, channel_multiplier=-1), writes=[MK])
                P.op("pool", lambda e: e.memset(mk[0][0:64, 64:128], 0.0), writes=[MK])
                P.op("pool", lambda e: e.memset(mk[1][:], 1.0), writes=[MK])
                P.op("pool", lambda e: e.affine_select(out=mk[1][:], in_=mk[1][:], pattern=[[-1, 128]], compare_op=ALU.is_ge, fill=0.0, base=0, channel_multiplier=1), writes=[MK])
                P.op("pool", lambda e: e.memset(mk[1][64:128, 0:64], 0.0), writes=[MK])
                P.op("pool", lambda e: e.memset(smk[0][:], 1.0), writes=[MK])
                P.op("pool", lambda e: e.memset(smk[0][:].rearrange("p (c k) -> p c k", k=64)[:, :, 0:1], 0.0), writes=[MK])
                P.op("pool", lambda e: e.memset(smk[1][:], 1.0), writes=[MK])
                P.op("pool", lambda e: e.memset(smk[1][:].rearrange("p (c k) -> p c k", k=64)[:, :, 63:64], 0.0), writes=[MK])
                lo = PP_OFF["lb_logits"]
                P.op("act", lambda e: e.activation(out=e01[:], in_=pp[:, lo:lo + 32], func=AF.Exp), reads=[B_const], writes=[MK])
                P.op("dve", lambda e: e.tensor_tensor(out=oml[:], in0=e01[:, 0:16], in1=e01[:, 16:32], op=ALU.add), writes=[MK])
                P.op("dve", lambda e: e.reciprocal(out=oml[:], in_=oml[:]), writes=[MK])
                if l == 0:
                    P.op("dve", lambda e: e.tensor_tensor(out=lbp[:], in0=e01[:, 0:16], in1=oml[:], op=ALU.mult), writes=[MK])
                    P.op("dve", lambda e: e.tensor_tensor(out=lbp[:], in0=lbp[:], in1=lbp[:], op=ALU.subtract), writes=[MK])
                else:
                    P.op("dve", lambda e: e.tensor_tensor(out=lbp[:], in0=e01[:, 16:32], in1=oml[:], op=ALU.mult), writes=[MK])
                P.op("dve", lambda e: e.tensor_scalar(out=oml[:], in0=lbp[:], scalar1=-1.0, scalar2=1.0, op0=ALU.mult, op1=ALU.add), writes=[MK])

                def v3(t):
                    return t[:].rearrange("p (c k) -> p c k", k=64)
                trbank = 0
                for hd in range(8):
                    ib = hd % 2
                    I = IN[ib]
                    BI = BIN[ib]
                    for ni, n in enumerate(inn):
                        P.dma("sync", I[n][:], uT[ni * A_W + hd * 128:ni * A_W + (hd + 1) * 128, :], reads=[B_uT], writes=[BI[n]], key=f"hg_{n}{ib}")
                    for g4 in range(4):
                        bk = 6 + (trbank % 2)
                        trbank += 1
                        for i in range(4):
                            tt = g4 * 4 + i
                            P.op("pe", lambda e, bk=bk, i=i, tt=tt, I=I: e.transpose(ps[bk][:, i * 128:(i + 1) * 128], I["iv"][:, tt * 128:(tt + 1) * 128], ident[:]),
                                 reads=[BI["iv"], B_const], writes=[PB[bk]])
                        P.op("act", lambda e, bk=bk, g4=g4: e.copy(out=vtok[:, g4 * 4:(g4 + 1) * 4, :], in_=ps[bk][:].rearrange("p (a n) -> p a n", a=4)),
                             reads=[PB[bk]], writes=[VT])
                    for d in range(2):
                        z = I["zf"] if d == 0 else I["zb"]
                        BZ = BI["zf"] if d == 0 else BI["zb"]
                        col = d * 8 + hd
                        F_, KK, LF, B_, E_ = T["F"], T["KK"], T["LF"], T["B"], T["E"]
                        P.op("act", lambda e, z=z: e.activation(out=F_[:], in_=z[:], func=AF.Sigmoid), reads=[BZ], writes=[Bf["F"]])
                        P.op("pool", lambda e, col=col: e.tensor_scalar(out=F_[:], in0=F_[:], scalar1=oml[:, col:col + 1], scalar2=lbp[:, col:col + 1], op0=ALU.mult, op1=ALU.add),
                             reads=[MK], writes=[Bf["F"]])
                        P.op("pool", lambda e: e.tensor_scalar(out=KK[:], in0=F_[:], scalar1=-1.0, scalar2=1.0, op0=ALU.mult, op1=ALU.add), reads=[Bf["F"]], writes=[Bf["KK"]])
                        P.op("act", lambda e: e.activation(out=LF[:], in_=F_[:], func=AF.Ln), reads=[Bf["F"]], writes=[Bf["LF"]])
                        if d == 0:
                            P.op("dve", lambda e: e.tensor_tensor_scan(out=B_[:], data0=smk[0][:], data1=LF[:], initial=0.0, op0=ALU.mult, op1=ALU.add),
                                 reads=[Bf["LF"], MK], writes=[Bf["B"]])
                        else:
                            P.op("dve", lambda e: e.tensor_tensor_scan(out=rev_ap(B_[:]), data0=rev_ap(smk[1][:]), data1=rev_ap(LF[:]), initial=0.0, op0=ALU.mult, op1=ALU.add),
                                 reads=[Bf["LF"], MK], writes=[Bf["B"]])
                        P.op("act", lambda e: e.activation(out=E_[:], in_=B_[:], func=AF.Exp), reads=[Bf["B"]], writes=[Bf["E"]])
                        P.op("dve", lambda e, I=I: e.tensor_tensor(out=Eb[:], in0=I["q"][:], in1=E_[:], op=ALU.mult), reads=[BI["q"], Bf["E"]], writes=[BEb])
                        P.op("act", lambda e: e.activation(out=F_[:], in_=B_[:], func=AF.Exp, scale=-1.0), reads=[Bf["B"], Bf["KK"], Bf["LF"]], writes=[Bf["F"]])
                        P.op("pool", lambda e: e.tensor_tensor(out=Fb[:], in0=KK[:], in1=F_[:], op=ALU.mult), reads=[Bf["KK"], Bf["F"]], writes=[BFb])
                        ecol = 63 if d == 0 else 0
                        P.op("dve", lambda e, ecol=ecol: e.tensor_tensor(out=v3(LF), in0=bcast_last(v3(B_)[:, :, ecol:ecol + 1], 64), in1=v3(B_), op=ALU.subtract),
                             reads=[Bf["B"]], writes=[Bf["LF"]])
                        P.op("act", lambda e: e.activation(out=LF[:], in_=LF[:], func=AF.Exp), writes=[Bf["LF"]])
                        P.op("pool", lambda e: e.tensor_tensor(out=LF[:], in0=KK[:], in1=LF[:], op=ALU.mult), reads=[Bf["KK"]], writes=[Bf["LF"]])
                        P.op("act", lambda e, ecol=ecol: e.activation(out=dec[:], in_=v3(B_)[:, :, ecol], func=AF.Exp), reads=[Bf["B"]], writes=[DEC])
                        for g4 in range(4):
                            bk = 6 + (trbank % 2)
                            trbank += 1
                            for i in range(4):
                                tt = g4 * 4 + i
                                P.op("pe", lambda e, bk=bk, i=i, tt=tt: e.transpose(ps[bk][:, i * 128:(i + 1) * 128], LF[:, tt * 128:(tt + 1) * 128], ident[:]),
                                     reads=[Bf["LF"], B_const], writes=[PB[bk]])
                            P.op("dve", lambda e, bk=bk, g4=g4: e.tensor_copy(out=khtok[:, g4 * 4:(g4 + 1) * 4, :], in_=ps[bk][:].rearrange("p (a n) -> p a n", a=4)),
                                 reads=[PB[bk]], writes=[KH])
                        P.op("pool", lambda e: e.memset(Sst[0][:], 0.0), writes=[SS[0]])
                        P.op("pool", lambda e: e.memset(Sb[0][:], 0.0), writes=[SSb[0]])
                        sstep = 0
                        prs = list(range(16)) if d == 0 else list(range(15, -1, -1))
                        ccs = (0, 1) if d == 0 else (1, 0)

                        def scores(pi):
                            pr = prs[pi]
                            sbk = pi % 2
                            tk = slice(pr * 128, (pr + 1) * 128)
                            P.op("pe", lambda e, sbk=sbk, tk=tk: e.matmul(ps[sbk][:, 0:128], Fb[:, tk], Eb[:, tk], start=True, stop=True),
                                 reads=[BFb, BEb], writes=[PB[sbk]])
                            P.op("dve", lambda e, sbk=sbk, d=d: e.tensor_tensor(out=stm[sbk][:], in0=ps[sbk][:, 0:128], in1=mk[d][:], op=ALU.mult),
                                 reads=[PB[sbk], MK], writes=[STM[sbk]])
                        scores(0)
                        for pi, pr in enumerate(prs):
                            if pi + 1 < 16:
                                scores(pi + 1)
                            sbk = pi % 2
                            obk = 2 + pi % 2
                            tk = slice(pr * 128, (pr + 1) * 128)
                            base = sstep
                            for ci, cc in enumerate(ccs):
                                c = pr * 2 + cc
                                ck = slice(cc * 64, (cc + 1) * 64)
                                dbk = 4 + (c % 2)
                                n_ = base + ci
                                fi, fo = n_ % 2, (n_ + 1) % 2
                                P.op("pe", lambda e, dbk=dbk, ck=ck, pr=pr: e.matmul(ps[dbk][:, 0:128], khtok[ck, pr, :], vtok[ck, pr, :], start=True, stop=True),
                                     reads=[KH, VT], writes=[PB[dbk]])
                                P.op("dve", lambda e, dbk=dbk, c=c, fi=fi, fo=fo: e.scalar_tensor_tensor(out=Sst[fo][:], in0=Sst[fi][:], scalar=dec[:, c:c + 1], in1=ps[dbk][:, 0:128], op0=ALU.mult, op1=ALU.add),
                                     reads=[PB[dbk], DEC, SS[fi]], writes=[SS[fo]])
                                P.op("act", lambda e, fo=fo, n_=n_: e.copy(out=Sb[(n_ + 1) % 4][:], in_=Sst[fo][:]), reads=[SS[fo]], writes=[SSb[(n_ + 1) % 4]])
                            for ci, cc in enumerate(ccs):
                                ck = slice(cc * 64, (cc + 1) * 64)
                                tck = slice(pr * 128 + cc * 64, pr * 128 + (cc + 1) * 64)
                                n_ = base + ci
                                P.op("pe", lambda e, obk=obk, ck=ck, pr=pr, sbk=sbk: e.matmul(ps[obk][:, ck], vtok[:, pr, :], stm[sbk][:, ck], start=True, stop=False),
                                     reads=[VT, STM[sbk]], writes=[PB[obk]])
                                P.op("pe", lambda e, obk=obk, ck=ck, tck=tck, n_=n_: e.matmul(ps[obk][:, ck], Sb[n_ % 4][:], Eb[:, tck], start=False, stop=True),
                                     reads=[SSb[n_ % 4], BEb], writes=[PB[obk]])
                            sstep = base + 2
                            if d == 0:
                                P.op("act", lambda e, obk=obk, tk=tk: e.copy(out=oacc[:, tk], in_=ps[obk][:, 0:128]), reads=[PB[obk]], writes=[OA])
                            else:
                                P.op("dve", lambda e, obk=obk, tk=tk: e.tensor_tensor(out=oacc[:, tk], in0=oacc[:, tk], in1=ps[obk][:, 0:128], op=ALU.add), reads=[PB[obk]], writes=[OA])
                    F_, KK, LF, B_, E_ = T["F"], T["KK"], T["LF"], T["B"], T["E"]
                    P.op("act", lambda e: e.activation(out=E_[:], in_=oacc[:], func=AF.Square), reads=[OA, Bf["F"]], writes=[Bf["E"]])
                    for j in range(4):
                        P.op("pe", lambda e, j=j: e.matmul(ps[j][:], ones_f[:], E_[:, j * 512:(j + 1) * 512], start=True, stop=True), reads=[Bf["E"], B_const], writes=[PB[j]])
                        P.op("act", lambda e, j=j: e.activation(out=F_[:, j * 512:(j + 1) * 512], in_=ps[j][:], func=AF.Sqrt, bias=RMS_EPS, scale=1.0 / 128), reads=[PB[j]], writes=[Bf["F"]])
                    P.op("dve", lambda e: e.reciprocal(out=F_[:], in_=F_[:]), writes=[Bf["F"]])
                    P.op("dve", lambda e: e.tensor_tensor(out=F_[:], in0=oacc[:], in1=F_[:], op=ALU.mult), reads=[OA], writes=[Bf["F"]])
                    P.op("act", lambda e, I=I: e.activation(out=KK[:], in_=I["g"][:], func=AF.Silu), reads=[BI["g"]], writes=[Bf["KK"]])
                    P.op("dve", lambda e, ib=ib, hd=hd: e.scalar_tensor_tensor(out=yb[ib][:], in0=F_[:], scalar=pc(f"hnw{l}", hd), in1=KK[:], op0=ALU.mult, op1=ALU.mult),
                         reads=[Bf["F"], Bf["KK"], B_const], writes=[YB[ib]])
                    P.dma("act", mixT[hd * 128:(hd + 1) * 128, :], yb[ib][:], reads=[YB[ib]], writes=[B_mixT], key=f"hg_y{ib}")
            P.barrier()

        def phase_attn(l):
            with ExitStack() as ph:
                def sb(name, shape, dt=F32):
                    return ph.enter_context(sbt(name, shape, dt))
                cosT = sb("at_cos", [128, S])
                sinT = sb("at_sin", [128, S])
                Rm = sb("at_R", [128, 128])
                CS = Buf()
                P.dma("sync", cosT[:], rope_d[0], writes=[CS], key="at_cos")
                P.dma("sync", sinT[:], rope_d[1], writes=[CS], key="at_sin")
                P.op("pool", lambda e: e.memset(Rm[:], 0.0), writes=[CS])
                for (c0, base, fill) in ((0, -32, -1.0), (32, 0, 1.0), (64, -96, -1.0), (96, -64, 1.0)):
                    P.op("pool", lambda e, c0=c0, base=base, fill=fill: e.affine_select(out=Rm[:, c0:c0 + 32], in_=Rm[:, c0:c0 + 32], pattern=[[-1, 32]],
                                                                                      compare_op=ALU.not_equal, fill=fill, base=base, channel_multiplier=1), writes=[CS])
                if w_down is not None:
                    wd_convert(sb, "at", l, 16, 32)
                xin = [sb(f"at_x{i}", [128, S]) for i in range(2)]
                XI = [Buf() for _ in range(2)]
                SQs = [sb(f"at_sq{i}", [128, S]) for i in range(2)]
                RSs = [sb(f"at_rs{i}", [128, S]) for i in range(2)]
                QNs = [sb(f"at_qn{i}", [128, S]) for i in range(2)]
                BSQs, BRSs, BQNs = [Buf(), Buf()], [Buf(), Buf()], [Buf(), Buf()]
                nrc = [0]
                KR = sb("at_kr", [128, S], BF16)
                BKR = Buf()
                QR = [sb(f"at_qr{i}", [128, S], BF16) for i in range(4)]
                BQR = [Buf() for _ in range(4)]
                vtok = sb("at_vtok", [128, 16, 128], BF16)
                VT = Buf()
                pT = [sb(f"at_p{i}", [128, 512], BF16) for i in range(4)]
                PT = [Buf() for _ in range(4)]
                SBK = (0, 1, 2, 7)
                rden = [sb(f"at_rd{i}", [128, 512]) for i in range(2)]
                RD = [Buf() for _ in range(2)]
                ost = [sb(f"at_o{i}", [128, 512], BF16) for i in range(2)]
                OS = [Buf() for _ in range(2)]
                c_base = A_COLS + B_COLS
                xctr = [0]

                def x_load(row0):
                    b = xctr[0] % 2
                    xctr[0] += 1
                    P.dma("sync", xin[b][:], uT[row0:row0 + 128, :], reads=[B_uT], writes=[XI[b]], key=f"at_x{b}")
                    return b

                def norm_rope(b, nwname, out_t, Bout):
                    X = xin[b]
                    ts_ = nrc[0] % 2
                    nrc[0] += 1
                    SQ, RS, QN = SQs[ts_], RSs[ts_], QNs[ts_]
                    BSQ, BRS, BQN = BSQs[ts_], BRSs[ts_], BQNs[ts_]
                    P.op("act", lambda e: e.activation(out=SQ[:], in_=X[:], func=AF.Square), reads=[XI[b]], writes=[BSQ])
                    for j in range(4):
                        P.op("pe", lambda e, j=j: e.matmul(ps[j][:], ones_f[:], SQ[:, j * 512:(j + 1) * 512], start=True, stop=True), reads=[BSQ, B_const], writes=[PB[j]])
                        P.op("act", lambda e, j=j: e.activation(out=RS[:, j * 512:(j + 1) * 512], in_=ps[j][:], func=AF.Sqrt, bias=RMS_EPS, scale=1.0 / 128), reads=[PB[j]], writes=[BRS])
                    P.op("dve", lambda e: e.reciprocal(out=RS[:], in_=RS[:]), writes=[BRS])
                    P.op("dve", lambda e: e.scalar_tensor_tensor(out=QN[:], in0=X[:], scalar=pc(nwname), in1=RS[:], op0=ALU.mult, op1=ALU.mult),
                         reads=[XI[b], BRS, B_const], writes=[BQN])
                    for j in range(4):
                        sl = slice(j * 512, (j + 1) * 512)
                        P.op("pe", lambda e, j=j, sl=sl: e.matmul(ps[4 + j][:], Rm[:], QN[:, sl], start=True, stop=True), reads=[BQN, CS], writes=[PB[4 + j]])
                        P.op("dve", lambda e, j=j, sl=sl: e.tensor_tensor(out=SQ[:, sl], in0=ps[4 + j][:], in1=sinT[:, sl], op=ALU.mult), reads=[PB[4 + j], CS], writes=[BSQ])
                    P.op("dve", lambda e: e.tensor_tensor(out=QN[:], in0=QN[:], in1=cosT[:], op=ALU.mult), reads=[CS], writes=[BQN])
                    P.op("dve", lambda e: e.tensor_tensor(out=out_t[:], in0=QN[:], in1=SQ[:], op=ALU.add), reads=[BQN, BSQ], writes=[Bout])

                scale = 128.0 ** -0.5
                octr = 0
                items = []
                for kv in range(4):
                    items.append(c_base + C_W + kv * 128)
                    items.append(c_base + C_W + 512 + kv * 128)
                    for g in range(4):
                        items.append(c_base + (kv * 4 + g) * 128)
                nxt = [0]
                pend = [x_load(items[0])]

                def take():
                    b = pend.pop(0)
                    nxt[0] += 1
                    if nxt[0] < len(items):
                        pend.append(x_load(items[nxt[0]]))
                    return b
                for kv in range(4):
                    norm_rope(take(), f"knw{l}", KR, BKR)
                    b = take()
                    for g4 in range(4):
                        bk = 4 + g4
                        for i in range(4):
                            tt = g4 * 4 + i
                            P.op("pe", lambda e, bk=bk, i=i, tt=tt, b=b: e.transpose(ps[bk][:, i * 128:(i + 1) * 128], xin[b][:, tt * 128:(tt + 1) * 128], ident[:]),
                                 reads=[XI[b], B_const], writes=[PB[bk]])
                        P.op("act", lambda e, bk=bk, g4=g4: e.copy(out=vtok[:, g4 * 4:(g4 + 1) * 4, :], in_=ps[bk][:].rearrange("p (a n) -> p a n", a=4)),
                             reads=[PB[bk]], writes=[VT])
                    for g in range(4):
                        norm_rope(take(), f"qnw{l}", QR[g], BQR[g])
                    for g in range(4):
                        hq = kv * 4 + g
                        qb = g
                        for qc in range(4):
                            qs = slice(qc * 512, (qc + 1) * 512)
                            ob = 3 + (octr % 2)
                            db = 5 + (octr % 2)
                            o2 = octr % 2
                            octr += 1

                            def s_mm(kt_, qb=qb, qs=qs):
                                pi_ = kt_ % 4
                                sbk = SBK[pi_]
                                P.op("pe", lambda e, sbk=sbk, kt_=kt_, qb=qb, qs=qs: e.matmul(ps[sbk][:], KR[:, kt_ * 128:(kt_ + 1) * 128], QR[qb][:, qs], start=True, stop=True),
                                     reads=[BKR, BQR[qb]], writes=[PB[sbk]])
                                P.op("act", lambda e, sbk=sbk, pi_=pi_: e.activation(out=pT[pi_][:], in_=ps[sbk][:], func=AF.Exp, scale=scale), reads=[PB[sbk]], writes=[PT[pi_]])

                            def pv_mm(kt_, ob=ob, db=db):
                                pi_ = kt_ % 4
                                P.op("pe", lambda e, pi_=pi_, kt_=kt_, ob=ob: e.matmul(ps[ob][:], vtok[:, kt_, :], pT[pi_][:], start=(kt_ == 0), stop=(kt_ == 15)),
                                     reads=[VT, PT[pi_]], writes=[PB[ob]])
                                P.op("pe", lambda e, pi_=pi_, kt_=kt_, db=db: e.matmul(ps[db][:], ones_b[:], pT[pi_][:], start=(kt_ == 0), stop=(kt_ == 15)),
                                     reads=[B_const, PT[pi_]], writes=[PB[db]])
                            for kt_ in range(3):
                                s_mm(kt_)
                            for kt_ in range(16):
                                if kt_ + 3 < 16:
                                    s_mm(kt_ + 3)
                                pv_mm(kt_)
                            P.op("dve", lambda e, o2=o2, db=db: e.reciprocal(out=rden[o2][:], in_=ps[db][:]), reads=[PB[db]], writes=[RD[o2]])
                            P.op("dve", lambda e, o2=o2, ob=ob: e.tensor_tensor(out=ost[o2][:], in0=ps[ob][:], in1=rden[o2][:], op=ALU.mult), reads=[PB[ob], RD[o2]], writes=[OS[o2]])
                            P.dma("sync", mixT[A_W + B_W + hq * 128:A_W + B_W + (hq + 1) * 128, qs], ost[o2][:], reads=[OS[o2]], writes=[B_mixT], key=f"at_o{o2}")
            P.barrier()

        if on("embed"):
            phase_embed()
        for l in layers:
            if on("win"):
                phase_win(l)
            if on("rglru"):
                phase_rglru(l)
            if on("hgrn"):
                phase_hgrn(l)
            if on("attn"):
                phase_attn(l)
            if on("wout"):
                phase_wout(l)
            if on("ln1"):
                phase_ln(f"ln1w{l}", f"ln1b{l}", False)
            if on("wup"):
                phase_wup(l)
            if on("wdown"):
                phase_wdown(l)
            if on("ln2"):
                phase_ln(f"ln2w{l}", f"ln2b{l}", l == DEPTH - 1 and phases is None)
        fw = [op for k, op in P.latest.items() if k.startswith("D_")]
        P.emit(final_waits=fw)
    return nc


_ROPE = None


def kernel(**inputs):
    global _ROPE
    n = 8
    nc = bass.Bass("TRN2", target_bir_lowering=False)
    build(nc)
    if _ROPE is None:
        _ROPE = _rope_tables()
    pp = _pack_params(inputs)
    f32 = lambda a: np.ascontiguousarray(np.asarray(a, np.float32))
    shared = {
        "emb_ln_w": f32(inputs["emb_ln_w"]), "emb_ln_b": f32(inputs["emb_ln_b"]),
        "w_in": f32(inputs["w_in"]), "w_out": f32(inputs["w_out"]),
        "ffn_w_up": f32(inputs["ffn_w_up"]), "ffn_w_down": f32(inputs["ffn_w_down"]),
        "rglru_wa": f32(inputs["rglru_wa"]), "rglru_wx": f32(inputs["rglru_wx"]),
        "pp": pp, "rope": _ROPE,
    }
    x = f32(inputs["x"])
    in_maps = [dict(shared, x=x[c]) for c in range(n)]
    res = run_bass_kernel_spmd(nc, in_maps, core_ids=list(range(n)))
    return np.stack([r["y"] for r in res.results], axis=0).astype(np.float32)
```
